# Optimizing a Trainium2 kernel written in Bass

```python
import jax, jax.numpy as jnp
from jax import lax
import numpy as np

D_MODEL = 1024
BATCH = 8
SEQ = 2048
DEPTH = 2
DEC_BATCH = 128
DEC_SEQ = 4
PAST_LEN = 16384
PAGE_SIZE = 128

CONV_CH = D_MODEL // 2
CONV_K = 31
RET_HEADS = 4
RET_DK = (D_MODEL // 2) // RET_HEADS
RET_DV = RET_DK
RET_QK = RET_HEADS * RET_DK
RET_WIDTH = RET_HEADS * RET_DV
MIX_WIDTH = CONV_CH + RET_WIDTH
MIX_IN = 2 * CONV_CH + 2 * RET_QK + 2 * RET_WIDTH
RET_CHUNK = 128
ROPE_BASE = 10000.0
D_FF = 4 * D_MODEL
N_MEM = 256
CA_HEADS = 4
CA_HEAD_DIM = D_MODEL // CA_HEADS
EPS = 1e-6
GN_EPS = 1e-5

kernel_name = "hymba_conformer_retnet_decoder_step"

F32 = jnp.float32


def rmsnorm(x, g):
    xf = x.astype(F32)
    y = xf * lax.rsqrt(jnp.mean(xf * xf, axis=-1, keepdims=True) + EPS) * g.astype(F32)
    return y.astype(x.dtype)


def swiglu(h, w1, w3, w2):
    return (jax.nn.silu(h @ w1) * (h @ w3)) @ w2


def rotary(t, pos):
    d = t.shape[-1]
    inv_freq = ROPE_BASE ** (-jnp.arange(0, d, 2, dtype=F32) / d)
    ang = pos[:, None] * inv_freq[None, :]
    cos = jnp.cos(ang)[None, :, None, :]
    sin = jnp.sin(ang)[None, :, None, :]
    t1, t2 = t[..., : d // 2], t[..., d // 2:]
    return jnp.concatenate([t1 * cos - t2 * sin, t1 * sin + t2 * cos], axis=-1)


def conv_mixer(a, gate, buf, conv_w, conv_b, ln_g, ln_b):
    u = (a.astype(F32) * jax.nn.sigmoid(gate.astype(F32)))
    full = jnp.concatenate([buf.astype(F32), u], axis=1)
    new_buf = full[:, -(CONV_K - 1):, :]
    y = lax.conv_general_dilated(
        full, conv_w.astype(F32)[:, None, :], window_strides=(1,), padding='VALID',
        dimension_numbers=('NWC', 'WIO', 'NWC'), feature_group_count=CONV_CH)
    y = y + conv_b.astype(F32)
    mu = jnp.mean(y, axis=-1, keepdims=True)
    var = jnp.mean(jnp.square(y - mu), axis=-1, keepdims=True)
    y = (y - mu) * lax.rsqrt(var + EPS) * ln_g.astype(F32) + ln_b.astype(F32)
    return jax.nn.silu(y).astype(a.dtype), new_buf.astype(buf.dtype)


def retention_chunkwise(q, k, v, s0, log_gamma):
    b, l, h, _ = q.shape
    c = RET_CHUNK if l % RET_CHUNK == 0 else l
    n = l // c
    idx = jnp.arange(c, dtype=F32)
    rel = idx[:, None] - idx[None, :]
    decay = jnp.where(rel[None] >= 0,
                      jnp.exp(log_gamma[:, None, None] * jnp.maximum(rel, 0.0)[None]), 0.0)
    q_dec = jnp.exp(log_gamma[:, None] * (idx[None, :] + 1.0))
    k_dec = jnp.exp(log_gamma[:, None] * (c - 1.0 - idx[None, :]))
    chunk_dec = jnp.exp(log_gamma * c)

    def to_chunks(t):
        return t.reshape(b, n, c, h, t.shape[-1]).transpose(1, 0, 3, 2, 4)

    def step(s, inp):
        qc, kc, vc = inp
        scores = jnp.einsum('bhid,bhjd->bhij', qc, kc) * decay[None]
        o = (jnp.einsum('bhij,bhjv->bhiv', scores, vc)
             + jnp.einsum('bhid,bhdv->bhiv', qc, s) * q_dec[None, :, :, None])
        s = (s * chunk_dec[None, :, None, None]
             + jnp.einsum('bhjd,bhjv->bhdv', kc * k_dec[None, :, :, None], vc))
        return s, o

    s, o = lax.scan(step, s0, (to_chunks(q), to_chunks(k), to_chunks(v)))
    o = o.transpose(1, 0, 3, 2, 4).reshape(b, l, h, v.shape[-1])
    return o, s


def retention_mixer(rq, rk, rv, rg, pos, s0, gn_g):
    b, l, _ = rq.shape
    log_gamma = jnp.log1p(-jnp.exp2(-5.0 - jnp.arange(RET_HEADS, dtype=F32)))
    q = rotary(rq.astype(F32).reshape(b, l, RET_HEADS, RET_DK), pos)
    k = rotary(rk.astype(F32).reshape(b, l, RET_HEADS, RET_DK), pos) * (RET_DK ** -0.5)
    v = rv.astype(F32).reshape(b, l, RET_HEADS, RET_DV)
    o, s = retention_chunkwise(q, k, v, s0.astype(F32), log_gamma)
    mu = jnp.mean(o, axis=-1, keepdims=True)
    var = jnp.mean(jnp.square(o - mu), axis=-1, keepdims=True)
    o = (o - mu) * lax.rsqrt(var + GN_EPS) * gn_g.astype(F32)[None, None]
    out = jax.nn.silu(rg.astype(F32)) * o.reshape(b, l, RET_WIDTH)
    return out.astype(rq.dtype), s.astype(s0.dtype)


def memory_kv(mem, g_mem, w_k, w_v):
    b = mem.shape[0]
    mn = rmsnorm(mem, g_mem)
    mk = (mn @ w_k).reshape(b, N_MEM, CA_HEADS, CA_HEAD_DIM)
    mv = (mn @ w_v).reshape(b, N_MEM, CA_HEADS, CA_HEAD_DIM)
    return mk, mv


def cross_attn(h, mem_k, mem_v, w_q, w_o):
    b, l, _ = h.shape
    q = (h @ w_q).reshape(b, l, CA_HEADS, CA_HEAD_DIM).astype(F32)
    s = jnp.einsum('blhd,bmhd->bhlm', q, mem_k.astype(F32)) * (CA_HEAD_DIM ** -0.5)
    p = jax.nn.softmax(s, axis=-1)
    o = jnp.einsum('bhlm,bmhd->blhd', p, mem_v.astype(F32)).reshape(b, l, D_MODEL)
    return o.astype(h.dtype) @ w_o


def trunk_layer(x, pos, conv_buf, ret_s, mem_k, mem_v,
                g_ffn1, w1a, w3a, w2a, g_mix, w_in, conv_w, conv_b, conv_ln_g, conv_ln_b,
                ret_gn_g, w_out, g_ca, w_cq, w_co, g_ffn2, w1b, w3b, w2b):
    x = x + 0.5 * swiglu(rmsnorm(x, g_ffn1), w1a, w3a, w2a)
    h = rmsnorm(x, g_mix)
    proj = h @ w_in
    cuts = [CONV_CH, 2 * CONV_CH, 2 * CONV_CH + RET_QK, 2 * CONV_CH + 2 * RET_QK,
            2 * CONV_CH + 2 * RET_QK + RET_WIDTH]
    c_a, c_g, r_q, r_k, r_v, r_g = jnp.split(proj, cuts, axis=-1)
    conv_out, new_buf = conv_mixer(c_a, c_g, conv_buf, conv_w, conv_b, conv_ln_g, conv_ln_b)
    ret_out, new_s = retention_mixer(r_q, r_k, r_v, r_g, pos, ret_s, ret_gn_g)
    x = x + jnp.concatenate([conv_out, ret_out], axis=-1) @ w_out
    x = x + cross_attn(rmsnorm(x, g_ca), mem_k, mem_v, w_cq, w_co)
    x = x + 0.5 * swiglu(rmsnorm(x, g_ffn2), w1b, w3b, w2b)
    return x, new_buf, new_s


def setup_inputs(seed: int = 0) -> dict:
    key = jax.random.key(seed)
    ks = iter(jax.random.split(key, 40))

    def nrm(shape, scale):
        return jax.random.normal(next(ks), shape, F32) * scale

    def gain(shape):
        return 1.0 + 0.05 * jax.random.normal(next(ks), shape, F32)

    return {
        "x_prompt": nrm((BATCH, SEQ, D_MODEL), 1.0),
        "x_sample": nrm((DEC_BATCH, DEC_SEQ, D_MODEL), 1.0),
        "state_conv": nrm((DEPTH, DEC_BATCH, CONV_K - 1, CONV_CH), 0.5),
        "state_ret": nrm((DEPTH, DEC_BATCH, RET_HEADS, RET_DK, RET_DV), 0.3),
        "cache_mem_k": nrm((DEPTH, DEC_BATCH, N_MEM, CA_HEADS, CA_HEAD_DIM), 1.0),
        "cache_mem_v": nrm((DEPTH, DEC_BATCH, N_MEM, CA_HEADS, CA_HEAD_DIM), 1.0),
        "mem_prompt": nrm((BATCH, N_MEM, D_MODEL), 1.0),
        "g_ffn1": gain((DEPTH, D_MODEL)),
        "w1_ffn1": nrm((DEPTH, D_MODEL, D_FF), D_MODEL ** -0.5),
        "w3_ffn1": nrm((DEPTH, D_MODEL, D_FF), D_MODEL ** -0.5),
        "w2_ffn1": nrm((DEPTH, D_FF, D_MODEL), D_FF ** -0.5),
        "g_mix": gain((DEPTH, D_MODEL)),
        "w_in": nrm((DEPTH, D_MODEL, MIX_IN), D_MODEL ** -0.5),
        "conv_w": nrm((DEPTH, CONV_K, CONV_CH), CONV_K ** -0.5),
        "conv_b": nrm((DEPTH, CONV_CH), 0.02),
        "conv_ln_g": gain((DEPTH, CONV_CH)),
        "conv_ln_b": nrm((DEPTH, CONV_CH), 0.02),
        "ret_gn_g": gain((DEPTH, RET_HEADS, RET_DV)),
        "w_out": nrm((DEPTH, MIX_WIDTH, D_MODEL), MIX_WIDTH ** -0.5),
        "g_ca": gain((DEPTH, D_MODEL)),
        "g_mem": gain((DEPTH, D_MODEL)),
        "w_cq": nrm((DEPTH, D_MODEL, D_MODEL), D_MODEL ** -0.5),
        "w_ck": nrm((DEPTH, D_MODEL, D_MODEL), D_MODEL ** -0.5),
        "w_cv": nrm((DEPTH, D_MODEL, D_MODEL), D_MODEL ** -0.5),
        "w_co": nrm((DEPTH, D_MODEL, D_MODEL), D_MODEL ** -0.5),
        "g_ffn2": gain((DEPTH, D_MODEL)),
        "w1_ffn2": nrm((DEPTH, D_MODEL, D_FF), D_MODEL ** -0.5),
        "w3_ffn2": nrm((DEPTH, D_MODEL, D_FF), D_MODEL ** -0.5),
        "w2_ffn2": nrm((DEPTH, D_FF, D_MODEL), D_FF ** -0.5),
        "g_final": gain((D_MODEL,)),
    }


def reference(x_prompt, x_sample, state_conv, state_ret, cache_mem_k, cache_mem_v, mem_prompt,
              g_ffn1, w1_ffn1, w3_ffn1, w2_ffn1, g_mix, w_in, conv_w, conv_b, conv_ln_g, conv_ln_b,
              ret_gn_g, w_out, g_ca, g_mem, w_cq, w_ck, w_cv, w_co,
              g_ffn2, w1_ffn2, w3_ffn2, w2_ffn2, g_final):
    b_p, l_p, _ = x_prompt.shape
    l_s = x_sample.shape[1]
    pos_p = jnp.arange(l_p, dtype=F32)
    pos_s = PAST_LEN + jnp.arange(l_s, dtype=F32)
    buf_p0 = jnp.zeros((b_p, CONV_K - 1, CONV_CH), x_prompt.dtype)
    ret_p0 = jnp.zeros((b_p, RET_HEADS, RET_DK, RET_DV), F32)

    xp, xs = x_prompt, x_sample
    conv_p, ret_p, memk_p, memv_p, conv_s, ret_s = [], [], [], [], [], []
    for l in range(DEPTH):
        lw = (g_ffn1[l], w1_ffn1[l], w3_ffn1[l], w2_ffn1[l], g_mix[l], w_in[l], conv_w[l], conv_b[l],
              conv_ln_g[l], conv_ln_b[l], ret_gn_g[l], w_out[l], g_ca[l], w_cq[l], w_co[l],
              g_ffn2[l], w1_ffn2[l], w3_ffn2[l], w2_ffn2[l])
        mk_p, mv_p = memory_kv(mem_prompt, g_mem[l], w_ck[l], w_cv[l])
        xp, bp, sp = trunk_layer(xp, pos_p, buf_p0, ret_p0, mk_p, mv_p, *lw)
        xs, bs, ss = trunk_layer(xs, pos_s, state_conv[l], state_ret[l],
                                 cache_mem_k[l], cache_mem_v[l], *lw)
        conv_p.append(bp); ret_p.append(sp); memk_p.append(mk_p); memv_p.append(mv_p)
        conv_s.append(bs); ret_s.append(ss)

    y_prompt = rmsnorm(xp, g_final)
    y_sample = rmsnorm(xs, g_final)
    new_conv_p = jnp.stack(conv_p)
    new_ret_p = jnp.stack(ret_p)
    new_memk_p = jnp.stack(memk_p)
    new_memv_p = jnp.stack(memv_p)
    new_conv_s = jnp.stack(conv_s)
    new_ret_s = jnp.stack(ret_s)
    return (y_prompt, y_sample, new_conv_p, new_ret_p, new_memk_p, new_memv_p, new_conv_s, new_ret_s)
```

```python
import contextlib
import os
import numpy as np
import concourse.bass as bass
import concourse.mybir as mybir
from concourse.bass_utils import run_bass_kernel_spmd

F32 = mybir.dt.float32
BF16 = mybir.dt.bfloat16
AF = mybir.ActivationFunctionType
ALU = mybir.AluOpType

D = 1024
SEQ = 2048
NB_S = 16
NS_TOK = 64
NT = SEQ + NS_TOK
DFF = 4096
NMEM = 256
CONV_K = 31
EPS = 1e-6
GN_EPS = 1e-5
PAST_LEN = 16384
TT = [(0, 512), (512, 512), (1024, 512), (1536, 512), (2048, 64)]
ENGS = ("pe", "act", "dve", "pool", "sp")
NRING = 3
ENG_COST = {"pe": 0.22, "act": 0.5, "dve": 0.75, "pool": 1.0, "sp": 0.1}
DMA_COST = 3.0
SEM_LAT = 0.2
WINDOW = {"pe": 640, "act": 96, "dve": 96, "pool": 8, "sp": 16}
REORDER = int(os.environ.get("KREORDER", "1"))


def PEC(n):
    return max(0.056, n / 2400.0 + 0.004)
KDBG = int(os.environ.get("KDBG", "99"))
KSUB = int(os.environ.get("KSUB", "99"))
SAME_ENGINE_SYNC = True


class LT:
    __slots__ = ("name", "w", "rs")

    def __init__(self, name=""):
        self.name = name
        self.w = None
        self.rs = []


class DSem:
    __slots__ = ("name", "h", "val")

    def __init__(self, name):
        self.name = name
        self.h = None
        self.val = 0


class Op:
    __slots__ = ("idx", "eng", "fn", "deps", "is_dma", "dsem", "dval", "ndma", "sig", "cnt", "cost")


class Sched:
    def __init__(self):
        self.ops = []
        self.dsems = []

    def dsem(self, name):
        d = DSem(name)
        self.dsems.append(d)
        return d

    def _add(self, eng, fn, r, w, is_dma, dsem, ndma, c=None):
        op = Op()
        op.cost = c if c is not None else (DMA_COST if is_dma else ENG_COST[eng])
        op.idx = len(self.ops)
        op.eng = eng
        op.fn = fn
        op.is_dma = is_dma
        op.dsem = dsem
        op.ndma = ndma
        op.sig = False
        op.cnt = None
        deps = set()
        for t in r:
            if t.w is not None:
                deps.add(t.w)
        for t in w:
            if t.w is not None:
                deps.add(t.w)
            deps.update(t.rs)
        op.deps = sorted(deps)
        if is_dma:
            dsem.val += 16 * ndma
            op.dval = dsem.val
        else:
            op.dval = None
        for t in w:
            t.w = op.idx
            t.rs = []
        for t in r:
            if t.w != op.idx:
                t.rs.append(op.idx)
        self.ops.append(op)
        return op

    def op(self, eng, fn, r=(), w=(), c=None):
        return self._add(eng, fn, r, w, False, None, 0, c)

    def dma(self, eng, fn, dsem, ndma, r=(), w=(), c=None):
        return self._add(eng, fn, r, w, True, dsem, ndma, c)

    def reorder(self):
        ops = self.ops
        n = len(ops)
        q = {e: [o.idx for o in ops if o.eng == e] for e in ENGS}
        head = {e: 0 for e in ENGS}
        done = [False] * n
        fin = [0.0] * n
        free = {e: 0.0 for e in ENGS}
        order = []
        cand = {e: None for e in ENGS}
        dirty = set(ENGS)

        def scan(e):
            qe = q[e]
            i = head[e]
            while i < len(qe) and done[qe[i]]:
                i += 1
            head[e] = i
            best = None
            cnt = 0
            W = WINDOW[e]
            fe = free[e]
            while i < len(qe) and cnt < W:
                oi = qe[i]
                i += 1
                if done[oi]:
                    continue
                cnt += 1
                ready = 0.0
                ok = True
                for d_ in ops[oi].deps:
                    if not done[d_]:
                        ok = False
                        break
                    if fin[d_] > ready:
                        ready = fin[d_]
                if not ok:
                    continue
                st_ = ready if ready > fe else fe
                if best is None or st_ < best[0] - 1e-9:
                    best = (st_, oi)
                    if st_ <= fe + 1e-9:
                        break
            return best

        left = n
        while left:
            for e in ENGS:
                cand[e] = scan(e)
            pick = None
            for e in ENGS:
                c_ = cand[e]
                if c_ is not None and (pick is None or c_[0] < pick[0]):
                    pick = (c_[0], c_[1], e)
            assert pick is not None, "scheduler deadlock"
            st_, oi, e = pick
            o = ops[oi]
            done[oi] = True
            if o.is_dma:
                free[e] = st_ + 0.08
                fin[oi] = st_ + o.cost
            else:
                free[e] = st_ + o.cost
                fin[oi] = st_ + o.cost + SEM_LAT
            order.append(oi)
            left -= 1
        self.order = order
        return max(fin)

    def lt_inherit(self, new, olds):
        acc = set(new.rs)
        for o in olds:
            if o.w is not None:
                acc.add(o.w)
            acc.update(o.rs)
        new.rs = sorted(acc)

    def emit(self, nc):
        ops = self.ops
        seq = [ops[i] for i in self.order] if getattr(self, "order", None) else ops
        per_eng = {e: [o for o in seq if o.eng == e] for e in ENGS}
        pos = {}
        for e in ENGS:
            for i, o in enumerate(per_eng[e]):
                pos[o.idx] = i
        need = {}
        for o in seq:
            best = {}
            dmas = {}
            for di in o.deps:
                p = ops[di]
                if p.is_dma:
                    k = id(p.dsem)
                    if k not in dmas or dmas[k].dval < p.dval:
                        dmas[k] = p
                    continue
                if p.eng == o.eng and (not SAME_ENGINE_SYNC or p.eng == "pe"):
                    continue
                if p.eng not in best or pos[best[p.eng].idx] < pos[p.idx]:
                    best[p.eng] = p
            for p in best.values():
                p.sig = True
            need[o.idx] = (list(best.values()), list(dmas.values()))
        cnt = {e: 0 for e in ENGS}
        for o in seq:
            if not o.is_dma and o.sig:
                cnt[o.eng] += 1
                o.cnt = cnt[o.eng]
        with contextlib.ExitStack() as st:
            esem = {e: st.enter_context(nc.semaphore("es_" + e)) for e in ENGS}
            for d in self.dsems:
                d.h = st.enter_context(nc.semaphore("ds_" + d.name))
            block = st.enter_context(nc.Block())

            def run(e, eng):
                waited = {}
                for o in per_eng[e]:
                    cps, dms = need[o.idx]
                    for p in cps:
                        key = ("e", p.eng)
                        if waited.get(key, 0) >= p.cnt:
                            continue
                        waited[key] = p.cnt
                        eng.wait_ge(esem[p.eng], p.cnt)
                    for p in dms:
                        key = ("d", id(p.dsem))
                        if waited.get(key, 0) >= p.dval:
                            continue
                        waited[key] = p.dval
                        eng.wait_ge(p.dsem.h, p.dval)
                    if o.is_dma:
                        o.fn(eng, o.dsem.h)
                    else:
                        ins = o.fn(eng)
                        if o.sig:
                            ins.then_inc(esem[e], 1)
                if e == "sp":
                    for d in self.dsems:
                        if d.val > 0:
                            eng.wait_ge(d.h, d.val)

            block.tensor(lambda eng: run("pe", eng))
            block.scalar(lambda eng: run("act", eng))
            block.vector(lambda eng: run("dve", eng))
            block.gpsimd(lambda eng: run("pool", eng))
            block.sync(lambda eng: run("sp", eng))
        return cnt


PCOL = {}
_c = 0
for _l in range(2):
    for _n, _w in (("g_ffn1", 8), ("g_mix", 8), ("g_ca", 8), ("g_mem", 8), ("g_ffn2", 8),
                   ("conv_b", 4), ("ln_g", 4), ("ln_b", 4), ("gn_g", 4), ("conv_w", 4 * CONV_K)):
        PCOL[(_n, _l)] = _c
        _c += _w
PCOL[("g_final", 0)] = _c
_c += 8
NPCOL = _c

TC_MASKP = 0
TC_QDEC = 512
TC_KDEC = 1024
TC_MASKS = 1028
TC_QDECS = 1284
TC_KDECS = 1540
TC_ONEH = 1544
NTCOL = 1560


def _fm(v):
    return np.ascontiguousarray(np.asarray(v, np.float32).reshape(-1, 128).T)


def pack_params(inp):
    P = np.zeros((128, NPCOL), np.float32)
    for l in range(2):
        for n, src in (("g_ffn1", "g_ffn1"), ("g_mix", "g_mix"), ("g_ca", "g_ca"), ("g_mem", "g_mem"),
                       ("g_ffn2", "g_ffn2"), ("conv_b", "conv_b"), ("ln_g", "conv_ln_g"), ("ln_b", "conv_ln_b")):
            a = _fm(inp[src][l])
            P[:, PCOL[(n, l)]:PCOL[(n, l)] + a.shape[1]] = a
        P[:, PCOL[("gn_g", l)]:PCOL[("gn_g", l)] + 4] = np.asarray(inp["ret_gn_g"][l], np.float32).T
        cw = np.asarray(inp["conv_w"][l], np.float32)
        cw = cw.reshape(CONV_K, 4, 128).transpose(2, 1, 0)
        P[:, PCOL[("conv_w", l)]:PCOL[("conv_w", l)] + 4 * CONV_K] = cw.reshape(128, 4 * CONV_K)
    P[:, PCOL[("g_final", 0)]:PCOL[("g_final", 0)] + 8] = _fm(inp["g_final"])
    return P


def const_tables():
    f32 = np.float32
    inv_freq = 10000.0 ** (-np.arange(0, 128, 2, dtype=np.float64) / 128.0)
    pos = np.concatenate([np.arange(SEQ, dtype=np.float64),
                          np.tile(float(PAST_LEN) + np.arange(4, dtype=np.float64), NB_S)])
    ang = pos[:, None] * inv_freq[None, :]
    cos = np.cos(ang).astype(f32).T
    sin = np.sin(ang).astype(f32).T
    cosT = np.concatenate([cos, cos], 0)
    sinT = np.concatenate([-sin, sin], 0)
    rope = np.ascontiguousarray(np.stack([cosT, sinT], 1)).astype(f32)

    T = np.zeros((128, NTCOL), np.float64)
    lg = np.log1p(-np.exp2(-5.0 - np.arange(4, dtype=np.float64)))
    sc = 128.0 ** -0.5
    j = np.arange(128)[:, None]
    i = np.arange(128)[None, :]
    for h in range(4):
        m = np.where(i >= j, np.exp(-lg[h] * (j + 1.0)), 0.0) * sc
        T[:, TC_MASKP + h * 128:TC_MASKP + (h + 1) * 128] = m
        T[:, TC_QDEC + h * 128:TC_QDEC + (h + 1) * 128] = np.exp(lg[h] * (i + 1.0))
        T[:, TC_KDEC + h] = np.exp(lg[h] * (127.0 - np.arange(128))) * sc
    js = np.arange(64)[:, None]
    is_ = np.arange(64)[None, :]
    lj, li = js % 4, is_ % 4
    same = (js // 4) == (is_ // 4)
    for h in range(4):
        m = np.where(same & (li >= lj), np.exp(-lg[h] * (lj + 1.0)), 0.0) * sc
        T[:64, TC_MASKS + h * 64:TC_MASKS + (h + 1) * 64] = m
        T[:, TC_QDECS + h * 64:TC_QDECS + (h + 1) * 64] = np.exp(lg[h] * (li + 1.0))
        T[:64, TC_KDECS + h] = np.exp(lg[h] * (3.0 - (np.arange(64) % 4))) * sc
    T[:64, TC_ONEH:TC_ONEH + 16] = (np.arange(64)[:, None] // 4 == np.arange(16)[None, :])
    mats = np.zeros((128, 256), f32)
    mats[:, 0:128] = np.eye(128, dtype=f32)
    k = np.arange(128)[:, None]
    d = np.arange(128)[None, :]
    mats[:, 128:256] = (k == (d + 64) % 128)
    gam = np.exp(lg)
    return rope, T.astype(f32), mats, gam


class Kern:
    def __init__(self, nc, s, st, wplan, stop_after):
        self.nc, self.s, self.st = nc, s, st
        self.wplan_in = wplan
        self.wplan_out = []
        self.wissued = 0
        self.wreleased = 0
        self.wcount = 0
        self.stop_after = stop_after
        self.gam = const_tables()[3]
        self._declare()

    def _declare(self):
        nc, st = self.nc, self.st

        def din(name, shape):
            return nc.dram_tensor(name, list(shape), F32, kind="ExternalInput").ap()

        def dout(name, shape):
            return nc.dram_tensor(name, list(shape), F32, kind="ExternalOutput").ap()

        self.d = d = {}
        d["xp"] = din("xp", (SEQ, D)); d["xs"] = din("xs", (NS_TOK, D))
        d["sconv"] = din("sconv", (2, NB_S, 30, 512)); d["sret"] = din("sret", (2, NB_S, 4, 128, 128))
        d["cmk"] = din("cmk", (2, NB_S, NMEM, D)); d["cmv"] = din("cmv", (2, NB_S, NMEM, D))
        d["memp"] = din("memp", (NMEM, D))
        for n, shp in (("w1a", (2, D, DFF)), ("w3a", (2, D, DFF)), ("w2a", (2, DFF, D)), ("w_in", (2, D, 3072)),
                       ("w_out", (2, D, D)), ("w_cq", (2, D, D)), ("w_ck", (2, D, D)), ("w_cv", (2, D, D)),
                       ("w_co", (2, D, D)), ("w1b", (2, D, DFF)), ("w3b", (2, D, DFF)), ("w2b", (2, DFF, D))):
            d[n] = din(n, shp)
        d["params"] = din("params", (128, NPCOL)); d["rope"] = din("rope", (128, 2, NT))
        d["tabs"] = din("tabs", (128, NTCOL)); d["mats"] = din("mats", (128, 256))
        d["yp"] = dout("yp", (SEQ, D)); d["ys"] = dout("ys", (NS_TOK, D))
        d["ncp"] = dout("ncp", (2, 30, 512)); d["nrp"] = dout("nrp", (2, 4, 128, 128))
        d["nmk"] = dout("nmk", (2, NMEM, D)); d["nmv"] = dout("nmv", (2, NMEM, D))
        d["ncs"] = dout("ncs", (2, NB_S, 30, 512)); d["nrs"] = dout("nrs", (2, NB_S, 4, 128, 128))

        def sb(name, shape, dt):
            return st.enter_context(nc.sbuf_tensor(name, list(shape), dt))

        self.xT = sb("xT", (128, 8, NT), F32)
        self.hT = sb("hT", (128, 8, NT), BF16)
        self.params = sb("params_sb", (128, NPCOL), F32)
        self.tabs = sb("tabs_sb", (128, NTCOL), F32)
        self.identf = sb("identf", (128, 128), F32)
        self.identb = sb("identb", (128, 128), BF16)
        self.rotP = sb("rotP", (128, 128), BF16)
        self.ones_rms = sb("ones_rms", (128, 128), BF16)
        self.ones_b = sb("ones_b", (128, 128), BF16)
        self.ones_ln = sb("ones_ln", (128, 128), F32)
        self.ones_gn = sb("ones_gn", (128, 128), F32)
        self.ring = [sb("ring%d" % i, (128, 4096), BF16) for i in range(NRING)]
        self.ARENA_BYTES = 74 * 1024
        self.arena = sb("arena", (128, self.ARENA_BYTES // 4), F32)
        self.psum = [st.enter_context(nc.psum_tensor("ps%d" % i, [128, 512], F32)) for i in range(8)]
        self.Lps = [LT("ps%d" % i) for i in range(8)]
        self.ps_i = 0
        self.LxT = [[LT("xT%d_%d" % (c, t)) for t in range(5)] for c in range(8)]
        self.LhT = [[LT("hT%d_%d" % (c, t)) for t in range(5)] for c in range(8)]
        self.Lring = [LT("ring%d" % i) for i in range(NRING)]
        self.Dring = [self.s.dsem("ring%d" % i) for i in range(NRING)]
        self.Lconst = LT("const")
        self.live = []
        self.retired = []
        self.aoff = 0

    def phase_reset(self, keep=0):
        self.retired.extend([x for x in self.live if x[0] >= keep])
        self.live = [x for x in self.live if x[0] < keep]
        self.aoff = keep

    def alloc(self, nbytes, nlt=1, name="a"):
        nbytes = (nbytes + 63) // 64 * 64
        off = self.aoff
        assert off + nbytes <= self.ARENA_BYTES, ("arena overflow", name, off, nbytes)
        self.aoff += nbytes
        lts = [LT("%s%d" % (name, i)) for i in range(nlt)]
        olds = []
        for (o, sz, ls) in self.retired:
            if o < off + nbytes and off < o + sz:
                olds.extend(ls)
        for n_ in lts:
            self.s.lt_inherit(n_, olds)
        self.live.append((off, nbytes, lts))
        return off, lts

    def vf32(self, off, n):
        return self.arena[:, off // 4: off // 4 + n]

    def vbf(self, off, n):
        return self.arena[:, off // 4: off // 4 + (n + 1) // 2].bitcast(BF16)[:, 0:n]

    def next_ps(self):
        i = self.ps_i
        self.ps_i = (i + 1) % 8
        return self.psum[i], self.Lps[i]

    def pc(self, name, l, c, n=1):
        o = PCOL[(name, l)] + c
        return self.params[:, o:o + n]

    def _issue_w(self, i, desc):
        slot = i % NRING
        ring, L, dsem = self.ring[slot], self.Lring[slot], self.Dring[slot]
        kind = desc[0]
        d = self.d
        if kind == "A":
            _, name, l, c0 = desc
            src = d[name][l][:, c0:c0 + 512].rearrange("(k p) n -> p k n", p=128)
            dst = ring[:].rearrange("p (k n) -> p k n", k=8)
            self.s.dma("pool", lambda e, h: e.dma_start(out=dst, in_=src).then_inc(h, 16), dsem, 1, w=[L], c=8.0)
        elif kind == "B":
            _, name, l, r0 = desc
            src = d[name][l][r0:r0 + 512, :].rearrange("(f p) n -> p f n", p=128)
            dst = ring[:].rearrange("p (f n) -> p f n", f=4)
            self.s.dma("pool", lambda e, h: e.dma_start(out=dst, in_=src).then_inc(h, 16), dsem, 1, w=[L], c=8.0)
        elif kind == "H":
            _, l, hh = desc
            dst = ring[:].rearrange("p (t k n) -> p t k n", t=4, k=8)

            def fn(e, h):
                for ty in range(4):
                    c0 = 1024 + ty * 512 + hh * 128
                    src = d["w_in"][l][:, c0:c0 + 128].rearrange("(k p) n -> p k n", p=128)
                    e.dma_start(out=dst[:, ty], in_=src).then_inc(h, 16)
            self.s.dma("pool", fn, dsem, 4, w=[L], c=8.0)

    def wblock(self, desc):
        i = self.wcount
        self.wcount += 1
        self.wplan_out.append(desc)
        assert i - NRING < self.wreleased, ("weight ring too small", i, self.wreleased)
        if self.wplan_in is not None:
            assert self.wplan_in[i] == desc
        while self.wissued <= i:
            self._issue_w(self.wissued, desc if self.wplan_in is None else self.wplan_in[self.wissued])
            self.wissued += 1
        slot = i % NRING
        return self.ring[slot], self.Lring[slot]

    def wrelease(self, n=1):
        self.wreleased += n
        assert self.wreleased <= self.wcount
        if self.wplan_in is not None:
            hi = min(len(self.wplan_in), self.wreleased + NRING)
            while self.wissued < hi:
                self._issue_w(self.wissued, self.wplan_in[self.wissued])
                self.wissued += 1

    def load_consts(self):
        s, d = self.s, self.d
        dsem = s.dsem("const")
        off_m, Lm = self.alloc(256 * 4, 1, "mats")
        mats = self.vf32(off_m, 256)

        def fn(e, h):
            e.dma_start(out=self.params[:], in_=d["params"]).then_inc(h, 16)
            e.dma_start(out=self.tabs[:], in_=d["tabs"]).then_inc(h, 16)
            e.dma_start(out=mats, in_=d["mats"]).then_inc(h, 16)
        s.dma("sp", fn, dsem, 3, w=[self.Lconst, Lm[0]])

        def cp(e):
            e.tensor_copy(out=self.identf[:], in_=mats[:, 0:128])
            e.tensor_copy(out=self.identb[:], in_=mats[:, 0:128])
            e.tensor_copy(out=self.rotP[:], in_=mats[:, 128:256])
            e.memset(self.ones_rms[:], 1.0 / 1024)
            e.memset(self.ones_b[:], 1.0)
            e.memset(self.ones_ln[:], 1.0 / 512)
            return e.memset(self.ones_gn[:], 1.0 / 128)
        s.op("dve", cp, r=[Lm[0]], w=[self.Lconst])

    def load_x(self):
        s, d = self.s, self.d
        self.phase_reset()
        offs, Ls = [], []
        for i in range(4):
            o, L = self.alloc(4096, 1, "xstage")
            offs.append(o); Ls.append(L[0])
        ds = [s.dsem("xst%d" % i) for i in range(4)]
        for tt in range(17):
            n = 128 if tt < 16 else 64
            sl = tt % 4
            stg = self.vf32(offs[sl], 1024)
            src = d["xp"][tt * 128:(tt + 1) * 128, :] if tt < 16 else d["xs"]
            s.dma("sp", lambda e, h, stg=stg, src=src, n=n: e.dma_start(out=stg[0:n, :], in_=src).then_inc(h, 16),
                  ds[sl], 1, w=[Ls[sl]])
            t = tt // 4 if tt < 16 else 4
            for half in range(2):
                ps, Lp = self.next_ps()
                for j in range(4):
                    c = half * 4 + j
                    s.op("pe", lambda e, ps=ps, j=j, c=c, stg=stg, n=n: e.transpose(
                        ps[:, j * 128:j * 128 + n], stg[0:n, c * 128:(c + 1) * 128], self.identf[0:n, 0:n]),
                        r=[Ls[sl], self.Lconst], w=[Lp], c=PEC(n))
                tok0 = tt * 128
                dst = self.xT[:, half * 4:half * 4 + 4, tok0:tok0 + n]
                srcp = ps[:].rearrange("p (c t) -> p c t", c=4)[:, :, 0:n]
                eng = "act" if half == 0 else "dve"
                if eng == "act":
                    s.op("act", lambda e, dst=dst, srcp=srcp: e.activation(out=dst, in_=srcp, func=AF.Copy),
                         r=[Lp], w=[self.LxT[c][t] for c in range(half * 4, half * 4 + 4)])
                else:
                    s.op("dve", lambda e, dst=dst, srcp=srcp: e.tensor_copy(out=dst, in_=srcp),
                         r=[Lp], w=[self.LxT[c][t] for c in range(half * 4, half * 4 + 4)])

    def rmsnorm(self, gname, l):
        s = self.s
        self.phase_reset()
        sq_o, sq_L = [], []
        for i in range(3):
            o, L = self.alloc(1024, 1, "sq")
            sq_o.append(o); sq_L.append(L[0])
        rs_o, rs_L = [], []
        for i in range(2):
            o, L = self.alloc(2048, 1, "rstd")
            rs_o.append(o); rs_L.append(L[0])
        qi = 0
        for t, (t0, n) in enumerate(TT):
            ps, Lp = self.next_ps()
            for c in range(8):
                sq = self.vbf(sq_o[qi % 3], 512)[:, 0:n]
                Lq = sq_L[qi % 3]
                qi += 1
                s.op("act", lambda e, sq=sq, c=c, t0=t0, n=n: e.activation(out=sq, in_=self.xT[:, c, t0:t0 + n], func=AF.Square),
                     r=[self.LxT[c][t]], w=[Lq])
                s.op("pe", lambda e, ps=ps, sq=sq, c=c, n=n: e.matmul(ps[:, 0:n], lhsT=self.ones_rms[:], rhs=sq, start=(c == 0), stop=(c == 7)),
                     r=[Lq, self.Lconst], w=[Lp], c=PEC(n))
            rstd = self.vf32(rs_o[t % 2], 512)[:, 0:n]
            Lr = rs_L[t % 2]
            s.op("act", lambda e, rstd=rstd, ps=ps, n=n: e.activation(out=rstd, in_=ps[:, 0:n], func=AF.Ln, bias=EPS, scale=1.0),
                 r=[Lp], w=[Lr])
            s.op("act", lambda e, rstd=rstd: e.activation(out=rstd, in_=rstd, func=AF.Exp, scale=-0.5), r=[Lr], w=[Lr])
            for c in range(8):
                s.op("dve", lambda e, c=c, t0=t0, n=n, rstd=rstd: e.scalar_tensor_tensor(
                    out=self.hT[:, c, t0:t0 + n], in0=self.xT[:, c, t0:t0 + n], scalar=self.pc(gname, l, c), in1=rstd,
                    op0=ALU.mult, op1=ALU.mult), r=[self.LxT[c][t], Lr, self.Lconst], w=[self.LhT[c][t]])

    def proj_acc(self, wdescs, in_fn, nk, scale):
        s = self.s
        for half in range(2):
            ring, Lw = self.wblock(wdescs[half])
            wv = ring[:].rearrange("p (k n) -> p k n", k=8)
            for t, (t0, n) in enumerate(TT):
                for dcl in range(4):
                    dc = half * 4 + dcl
                    ps, Lp = self.next_ps()
                    for k in range(nk):
                        rhs, Lr = in_fn(k, t)
                        s.op("pe", lambda e, ps=ps, wv=wv, k=k, dcl=dcl, rhs=rhs, n=n: e.matmul(
                            ps[:, 0:n], lhsT=wv[:, k, dcl * 128:(dcl + 1) * 128], rhs=rhs, start=(k == 0), stop=(k == nk - 1)),
                            r=[Lw, Lr], w=[Lp], c=PEC(n))
                    s.op("dve", lambda e, ps=ps, dc=dc, t0=t0, n=n: e.scalar_tensor_tensor(
                        out=self.xT[:, dc, t0:t0 + n], in0=ps[:, 0:n], scalar=float(scale), in1=self.xT[:, dc, t0:t0 + n],
                        op0=ALU.mult, op1=ALU.add), r=[Lp, self.LxT[dc][t]], w=[self.LxT[dc][t]])
            self.wrelease(1)

    def ffn(self, l, which):
        s = self.s
        self.phase_reset()
        w1, w3, w2 = "w1" + which, "w3" + which, "w2" + which
        g_o, g_L = self.alloc(4 * NT * 2, 20, "gT")
        gT = self.vbf(g_o, 4 * NT).rearrange("p (f t) -> p f t", f=4)
        sg_o, sg_L = [], []
        for i in range(2):
            o, L = self.alloc(2048, 1, "sg")
            sg_o.append(o); sg_L.append(L[0])
        si = 0
        for fg in range(8):
            r1, L1 = self.wblock(("A", w1, l, fg * 512))
            r3, L3 = self.wblock(("A", w3, l, fg * 512))
            r2, L2 = self.wblock(("B", w2, l, fg * 512))
            v1 = r1[:].rearrange("p (k n) -> p k n", k=8)
            v3 = r3[:].rearrange("p (k n) -> p k n", k=8)
            v2 = r2[:].rearrange("p (f n) -> p f n", f=4)
            for fc in range(4):
                for t, (t0, n) in enumerate(TT):
                    pa, La = self.next_ps()
                    pb, Lb = self.next_ps()
                    for (pp, Lpp, vv, Lw) in ((pa, La, v1, L1), (pb, Lb, v3, L3)):
                        for k in range(8):
                            s.op("pe", lambda e, pp=pp, vv=vv, k=k, fc=fc, t0=t0, n=n: e.matmul(
                                pp[:, 0:n], lhsT=vv[:, k, fc * 128:(fc + 1) * 128], rhs=self.hT[:, k, t0:t0 + n],
                                start=(k == 0), stop=(k == 7)), r=[Lw, self.LhT[k][t]], w=[Lpp], c=PEC(n))
                    sg = self.vf32(sg_o[si % 2], 512)[:, 0:n]
                    Ls = sg_L[si % 2]
                    si += 1
                    s.op("act", lambda e, sg=sg, pa=pa, n=n: e.activation(out=sg, in_=pa[:, 0:n], func=AF.Silu), r=[La], w=[Ls])
                    s.op("dve", lambda e, sg=sg, pb=pb, fc=fc, t0=t0, n=n: e.tensor_tensor(
                        out=gT[:, fc, t0:t0 + n], in0=sg, in1=pb[:, 0:n], op=ALU.mult), r=[Ls, Lb], w=[g_L[fc * 5 + t]])
            self.wrelease(2)
            for t, (t0, n) in enumerate(TT):
                for dc in range(8):
                    ps, Lp = self.next_ps()
                    for fc in range(4):
                        s.op("pe", lambda e, ps=ps, fc=fc, dc=dc, t0=t0, n=n, v2=v2: e.matmul(
                            ps[:, 0:n], lhsT=v2[:, fc, dc * 128:(dc + 1) * 128], rhs=gT[:, fc, t0:t0 + n],
                            start=(fc == 0), stop=(fc == 3)), r=[L2, g_L[fc * 5 + t]], w=[Lp], c=PEC(n))
                    s.op("dve", lambda e, ps=ps, dc=dc, t0=t0, n=n: e.scalar_tensor_tensor(
                        out=self.xT[:, dc, t0:t0 + n], in0=ps[:, 0:n], scalar=0.5, in1=self.xT[:, dc, t0:t0 + n],
                        op0=ALU.mult, op1=ALU.add), r=[Lp, self.LxT[dc][t]], w=[self.LxT[dc][t]])
            self.wrelease(1)

    def conv(self, l):
        s, d = self.s, self.d
        self.phase_reset()
        UP = SEQ + 30
        up_o, up_L = self.alloc(4 * UP * 2, 4, "up")
        us_o, us_L = self.alloc(4 * 16 * 34 * 4, 4, "us")
        u32_o, u32_L = self.alloc(4 * 30 * 4, 1, "u32p")
        up = self.vbf(up_o, 4 * UP).rearrange("p (c t) -> p c t", c=4)
        us = self.vf32(us_o, 4 * 16 * 34).rearrange("p (c b t) -> p c b t", c=4, b=16)
        u32p = self.vf32(u32_o, 120).rearrange("p (c t) -> p c t", c=4)
        sg_o, sg_L = [], []
        for i in range(2):
            o, L = self.alloc(2048, 1, "sgm")
            sg_o.append(o); sg_L.append(L[0])
        st2_o, st2_L = self.alloc(2048, 1, "cstage2")
        st2 = self.vf32(st2_o, 512)
        y_o, y_L = self.alloc(4 * NT * 4, 20, "y")
        y = self.vf32(y_o, 4 * NT).rearrange("p (c t) -> p c t", c=4)
        keep = self.aoff
        stg_o, stg_L = self.alloc(4 * 512 * 4, 1, "cstage")
        stg = self.vf32(stg_o, 2048).rearrange("p (g c) -> p g c", g=4)
        dst_in = s.dsem("cst_in%d" % l)
        dst_out = s.dsem("cst_out%d" % l)
        dst_out2 = s.dsem("cst_outp%d" % l)
        src = d["sconv"][l].rearrange("(g bb) t c -> (bb t) g c", bb=4)
        s.dma("sp", lambda e, h: e.dma_start(out=stg[0:120], in_=src).then_inc(h, 16), dst_in, 1, w=[stg_L[0]])
        for cc in range(4):
            ps, Lp = self.next_ps()
            for g in range(4):
                s.op("pe", lambda e, ps=ps, g=g, cc=cc: e.transpose(ps[:, g * 120:(g + 1) * 120], stg[0:120, g, cc * 128:(cc + 1) * 128],
                                                                   self.identf[0:120, 0:120]), r=[stg_L[0], self.Lconst], w=[Lp])
            s.op("act", lambda e, ps=ps, cc=cc: e.activation(out=us[:, cc, :, 0:30], in_=ps[:, 0:480].rearrange("p (b t) -> p b t", b=16),
                                                             func=AF.Copy), r=[Lp], w=[us_L[cc]])
        s.op("dve", lambda e: e.memset(up[:, :, 0:30], 0.0), w=up_L)
        ra, La_ = self.wblock(("A", "w_in", l, 0))
        rg, Lg_ = self.wblock(("A", "w_in", l, 512))
        va = ra[:].rearrange("p (k n) -> p k n", k=8)
        vg = rg[:].rearrange("p (k n) -> p k n", k=8)
        si = 0
        for cc in range(4):
            for t, (t0, n) in enumerate(TT):
                pa, La = self.next_ps()
                pg, Lg = self.next_ps()
                for (pp, Lpp, vv, Lw) in ((pa, La, va, La_), (pg, Lg, vg, Lg_)):
                    for k in range(8):
                        s.op("pe", lambda e, pp=pp, vv=vv, k=k, cc=cc, t0=t0, n=n: e.matmul(
                            pp[:, 0:n], lhsT=vv[:, k, cc * 128:(cc + 1) * 128], rhs=self.hT[:, k, t0:t0 + n],
                            start=(k == 0), stop=(k == 7)), r=[Lw, self.LhT[k][t]], w=[Lpp], c=PEC(n))
                sg = self.vf32(sg_o[si % 2], 512)[:, 0:n]
                Ls = sg_L[si % 2]
                si += 1
                s.op("act", lambda e, sg=sg, pg=pg, n=n: e.activation(out=sg, in_=pg[:, 0:n], func=AF.Sigmoid), r=[Lg], w=[Ls])
                if t < 4:
                    s.op("dve", lambda e, sg=sg, pa=pa, cc=cc, t0=t0: e.tensor_tensor(
                        out=up[:, cc, 30 + t0:30 + t0 + 512], in0=pa[:, 0:512], in1=sg, op=ALU.mult), r=[La, Ls], w=[up_L[cc]])
                    if t == 3:
                        s.op("dve", lambda e, sg=sg, pa=pa, cc=cc: e.tensor_tensor(
                            out=u32p[:, cc, :], in0=pa[:, 482:512], in1=sg[:, 482:512], op=ALU.mult), r=[La, Ls], w=[u32_L[0]])
                else:
                    s.op("dve", lambda e, sg=sg, pa=pa, cc=cc: e.tensor_tensor(
                        out=us[:, cc, :, 30:34], in0=pa[:, 0:64].rearrange("p (b t) -> p b t", b=16),
                        in1=sg.rearrange("p (b t) -> p b t", b=16), op=ALU.mult), r=[La, Ls], w=[us_L[cc]])
        self.wrelease(2)
        psg = [self.next_ps() for g in range(4)]
        for cc in range(4):
            cont = self.vf32(sg_o[cc % 2], 512)[:, 0:480]
            Lc = sg_L[cc % 2]
            s.op("dve", lambda e, cont=cont, cc=cc: e.tensor_copy(out=cont.rearrange("p (b t) -> p b t", b=16), in_=us[:, cc, :, 4:34]),
                 r=[us_L[cc]], w=[Lc])
            for g in range(4):
                ps, Lp = psg[g]
                s.op("pe", lambda e, ps=ps, g=g, cc=cc, cont=cont: e.transpose(ps[0:120, cc * 128:(cc + 1) * 128], cont[:, g * 120:(g + 1) * 120],
                                                                              self.identf[:]), r=[Lc, self.Lconst], w=[Lp])
        for g in range(4):
            ps, Lp = psg[g]
            s.op("act", lambda e, ps=ps, g=g: e.activation(out=stg[0:120, g, :], in_=ps[0:120, :], func=AF.Copy), r=[Lp], w=[stg_L[0]])
        dsto = d["ncs"][l].rearrange("(g bb) t c -> (bb t) g c", bb=4)
        s.dma("sp", lambda e, h: e.dma_start(out=dsto, in_=stg[0:120]).then_inc(h, 16), dst_out, 1, r=[stg_L[0]], w=[LT()])
        ps, Lp = self.next_ps()
        for cc in range(4):
            s.op("pe", lambda e, ps=ps, cc=cc: e.transpose(ps[0:30, cc * 128:(cc + 1) * 128], u32p[:, cc, :], self.identf[:]),
                 r=[u32_L[0], self.Lconst], w=[Lp])
        s.op("act", lambda e, ps=ps: e.activation(out=st2[0:30, :], in_=ps[0:30, :], func=AF.Copy), r=[Lp], w=[st2_L[0]])
        s.dma("sp", lambda e, h: e.dma_start(out=d["ncp"][l], in_=st2[0:30, :]).then_inc(h, 16), dst_out2, 1, r=[st2_L[0]], w=[LT()])
        cw0 = PCOL[("conv_w", l)]
        self.phase_reset(keep=keep)
        dg_o, dg_L = self.alloc(CONV_K * 128 * 2, 1, "diag")
        diag = self.vbf(dg_o, CONV_K * 128).rearrange("p (j d) -> p j d", j=CONV_K)
        for cc in range(4):
            s.op("dve", lambda e, cc=cc: e.tensor_tensor(
                out=diag, in0=self.identb[:].unsqueeze(1).to_broadcast([128, CONV_K, 128]),
                in1=self.params[:, cw0 + cc * CONV_K:cw0 + (cc + 1) * CONV_K].unsqueeze(2).to_broadcast([128, CONV_K, 128]), op=ALU.mult),
                r=[self.Lconst], w=[dg_L[0]])
            for t in range(4):
                t0 = TT[t][0]
                ps, Lp = self.next_ps()
                for j in range(CONV_K):
                    s.op("pe", lambda e, ps=ps, j=j, cc=cc, t0=t0: e.matmul(ps[:], lhsT=diag[:, j, :], rhs=up[:, cc, t0 + j:t0 + j + 512],
                                                                         start=(j == 0), stop=(j == CONV_K - 1)), r=[dg_L[0], up_L[cc]], w=[Lp])
                s.op("act", lambda e, ps=ps, cc=cc, t0=t0: e.activation(out=y[:, cc, t0:t0 + 512], in_=ps[:], func=AF.Identity,
                                                                      bias=self.pc("conv_b", l, cc), scale=1.0), r=[Lp, self.Lconst], w=[y_L[cc * 5 + t]])
            yv = y[:, cc, SEQ:SEQ + 64].rearrange("p (b t) -> p b t", b=16)
            Ly = y_L[cc * 5 + 4]
            s.op("dve", lambda e, yv=yv, cc=cc: e.tensor_scalar(
                out=yv, in0=us[:, cc, :, 0:4], scalar1=self.params[:, cw0 + cc * CONV_K:cw0 + cc * CONV_K + 1],
                scalar2=self.pc("conv_b", l, cc), op0=ALU.mult, op1=ALU.add), r=[us_L[cc], self.Lconst], w=[Ly])
            for j in range(1, CONV_K):
                s.op("dve", lambda e, yv=yv, cc=cc, j=j: e.scalar_tensor_tensor(
                    out=yv, in0=us[:, cc, :, j:j + 4], scalar=self.params[:, cw0 + cc * CONV_K + j:cw0 + cc * CONV_K + j + 1], in1=yv,
                    op0=ALU.mult, op1=ALU.add), r=[us_L[cc], self.Lconst], w=[Ly])
        self.retired.append((up_o, 4 * UP * 2, up_L))
        self.retired.append((us_o, 4 * 16 * 34 * 4, us_L))
        co_L = [LT("co%d" % i) for i in range(20)]
        for n_ in co_L:
            self.s.lt_inherit(n_, up_L + us_L)
        co = self.vbf(up_o, 4 * NT).rearrange("p (c t) -> p c t", c=4)
        assert 4 * NT * 2 <= 4 * UP * 2 + 4 * 16 * 34 * 4
        self.phase_reset(keep=keep)
        tm_o, tm_L = [], []
        for i in range(4):
            o, L = self.alloc(2048, 1, "lnt")
            tm_o.append(o); tm_L.append(L[0])
        for t, (t0, n) in enumerate(TT):
            pm, Lm = self.next_ps()
            pe2, Le = self.next_ps()
            for cc in range(4):
                sq = self.vf32(sg_o[cc % 2], 512)[:, 0:n]
                Lq = sg_L[cc % 2]
                s.op("act", lambda e, sq=sq, cc=cc, t0=t0, n=n: e.activation(out=sq, in_=y[:, cc, t0:t0 + n], func=AF.Square),
                     r=[y_L[cc * 5 + t]], w=[Lq])
                s.op("pe", lambda e, pm=pm, cc=cc, t0=t0, n=n: e.matmul(pm[:, 0:n], lhsT=self.ones_ln[:], rhs=y[:, cc, t0:t0 + n],
                                                                       start=(cc == 0), stop=(cc == 3)), r=[y_L[cc * 5 + t], self.Lconst], w=[Lm], c=PEC(n))
                s.op("pe", lambda e, pe2=pe2, sq=sq, cc=cc, n=n: e.matmul(pe2[:, 0:n], lhsT=self.ones_ln[:], rhs=sq,
                                                                          start=(cc == 0), stop=(cc == 3)), r=[Lq, self.Lconst], w=[Le], c=PEC(n))
            mean, rstd, z = [self.vf32(tm_o[i], 512)[:, 0:n] for i in range(3)]
            Lmean, Lrstd, Lz = tm_L[0], tm_L[1], tm_L[2]
            self.norm_stats(pm, Lm, pe2, Le, mean, Lmean, rstd, Lrstd, n, EPS)
            for cc in range(4):
                s.op("dve", lambda e, z=z, mean=mean, cc=cc, t0=t0, n=n: e.tensor_tensor(out=z, in0=y[:, cc, t0:t0 + n], in1=mean, op=ALU.subtract),
                     r=[y_L[cc * 5 + t], Lmean], w=[Lz])
                s.op("dve", lambda e, z=z, rstd=rstd: e.tensor_tensor(out=z, in0=z, in1=rstd, op=ALU.mult), r=[Lrstd], w=[Lz])
                s.op("act", lambda e, z=z, cc=cc, t0=t0, n=n: e.activation(out=co[:, cc, t0:t0 + n], in_=z, func=AF.Silu,
                                                                           bias=self.pc("ln_b", l, cc), scale=self.pc("ln_g", l, cc)),
                     r=[Lz, self.Lconst], w=[co_L[cc * 5 + t]])
        self.proj_acc_rows("w_out", l, 0, lambda k, t: (co[:, k, TT[t][0]:TT[t][0] + TT[t][1]], co_L[k * 5 + t]), 1.0)
        self.retired.append((up_o, 4 * NT * 2, co_L))

    def norm_stats(self, pm, Lm, pe2, Le, mean, Lmean, rstd, Lrstd, n, eps):
        s = self.s
        s.op("act", lambda e: e.activation(out=mean, in_=pm[:, 0:n], func=AF.Copy), r=[Lm], w=[Lmean])
        s.op("dve", lambda e: e.tensor_tensor(out=rstd, in0=mean, in1=mean, op=ALU.mult), r=[Lmean], w=[Lrstd])
        s.op("dve", lambda e: e.tensor_tensor(out=rstd, in0=pe2[:, 0:n], in1=rstd, op=ALU.subtract), r=[Le], w=[Lrstd])
        s.op("act", lambda e: e.activation(out=rstd, in_=rstd, func=AF.Ln, bias=float(eps), scale=1.0), r=[Lrstd], w=[Lrstd])
        s.op("act", lambda e: e.activation(out=rstd, in_=rstd, func=AF.Exp, scale=-0.5), r=[Lrstd], w=[Lrstd])

    def proj_acc_rows(self, wname, l, r0, in_fn, scale):
        s = self.s
        ring, Lw = self.wblock(("B", wname, l, r0))
        wv = ring[:].rearrange("p (f n) -> p f n", f=4)
        for t, (t0, n) in enumerate(TT):
            for dc in range(8):
                ps, Lp = self.next_ps()
                for k in range(4):
                    rhs, Lr = in_fn(k, t)
                    s.op("pe", lambda e, ps=ps, k=k, dc=dc, rhs=rhs, n=n: e.matmul(
                        ps[:, 0:n], lhsT=wv[:, k, dc * 128:(dc + 1) * 128], rhs=rhs, start=(k == 0), stop=(k == 3)),
                        r=[Lw, Lr], w=[Lp], c=PEC(n))
                s.op("dve", lambda e, ps=ps, dc=dc, t0=t0, n=n: e.scalar_tensor_tensor(
                    out=self.xT[:, dc, t0:t0 + n], in0=ps[:, 0:n], scalar=float(scale), in1=self.xT[:, dc, t0:t0 + n],
                    op0=ALU.mult, op1=ALU.add), r=[Lp, self.LxT[dc][t]], w=[self.LxT[dc][t]])
        self.wrelease(1)

    def retention(self, l):
        s, d = self.s, self.d
        self.phase_reset()
        ro_o, ro_L = self.alloc(4 * NT * 2, 20, "retout")
        ro = self.vbf(ro_o, 4 * NT).rearrange("p (c t) -> p c t", c=4)
        rp_o, rp_L = [], []
        for i in range(2):
            o, L = self.alloc(2 * 512 * 4, 1, "rope")
            rp_o.append(o); rp_L.append(L[0])
        drope = [s.dsem("rope%d_%d" % (l, i)) for i in range(2)]
        q_o, q_L = self.alloc(NT * 2, 5, "qT")
        k_o, k_L = self.alloc(NT * 2, 5, "kT")
        gt_o, gt_L = self.alloc(NT * 2, 5, "gate")
        v_o, v_L = self.alloc(17 * 128 * 2, 5, "vtok")
        kd_o, kd_L = self.alloc(17 * 128 * 2, 5, "kdtok")
        s32_o, s32_L = self.alloc(128 * 4, 1, "S32")
        s0_o, s0_L = self.alloc(16 * 128 * 4, 1, "S0")
        q32_o, q32_L = self.alloc(64 * 4, 1, "q32s")
        qT = self.vbf(q_o, NT); kT = self.vbf(k_o, NT); gate = self.vbf(gt_o, NT)
        vtok = self.vbf(v_o, 17 * 128).rearrange("p (n d) -> p n d", n=17)
        kdtok = self.vbf(kd_o, 17 * 128).rearrange("p (n d) -> p n d", n=17)
        S32 = self.vf32(s32_o, 128)
        S0 = self.vf32(s0_o, 16 * 128).rearrange("p (b d) -> p b d", b=16)
        q32s = self.vf32(q32_o, 64)
        tmp_o, tmp_L = [], []
        for i in range(5):
            o, L = self.alloc(2048, 1, "rtmp")
            tmp_o.append(o); tmp_L.append(L[0])
        bt_o, bt_L = [], []
        for i in range(2):
            o, L = self.alloc(1024, 1, "rbt")
            bt_o.append(o); bt_L.append(L[0])
        st_o, st_L = [], []
        for i in range(2):
            o, L = self.alloc(1024, 1, "sTsb")
            st_o.append(o); st_L.append(L[0])
        keep = self.aoff
        ri = 0
        ds0 = s.dsem("s0in%d" % l)
        dso = s.dsem("soutp%d" % l)
        dso_s = s.dsem("souts%d" % l)
        tabs = self.tabs
        bi = 0
        for hh in range(4):
            if KDBG < 0:
                break
            ring, Lw = self.wblock(("H", l, hh))
            wv = ring[:].rearrange("p (t k n) -> p t k n", t=4, k=8)
            self.phase_reset(keep=keep)
            sb_o, sb_L = self.alloc(16 * 128 * 2, 16, "Sbf")
            Sbf = self.vbf(sb_o, 16 * 128).rearrange("p (n d) -> p n d", n=16)
            def ld_s0(e, h, hh=hh):
                for bg in range(4):
                    e.dma_start(out=S0[:, 4 * bg:4 * bg + 4, :], in_=d["sret"][l][4 * bg:4 * bg + 4, hh].rearrange("b k v -> k b v")).then_inc(h, 16)
            s.dma("sp", ld_s0, ds0, 4, w=[s0_L[0]])
            for t, (t0, n) in enumerate(TT):
                rope = self.vf32(rp_o[ri % 2], 1024).rearrange("p (a t) -> p a t", a=2)
                Lrope = rp_L[ri % 2]
                s.dma("sp", lambda e, h, rope=rope, t0=t0, n=n: e.dma_start(out=rope[:, :, 0:n], in_=d["rope"][:, :, t0:t0 + n]).then_inc(h, 16),
                      drope[ri % 2], 1, w=[Lrope])
                ri += 1
                if KDBG < 1:
                    continue
                for ty, dstT, dL in ((0, qT, q_L), (1, kT, k_L)):
                    ps, Lp = self.next_ps()
                    for k in range(8):
                        s.op("pe", lambda e, ps=ps, ty=ty, k=k, t0=t0, n=n, wv=wv: e.matmul(
                            ps[:, 0:n], lhsT=wv[:, ty, k, :], rhs=self.hT[:, k, t0:t0 + n], start=(k == 0), stop=(k == 7)),
                            r=[Lw, self.LhT[k][t]], w=[Lp], c=PEC(n))
                    if KSUB < 2:
                        continue
                    qb = self.vbf(bt_o[bi % 2], 512)[:, 0:n]
                    Lqb = bt_L[bi % 2]
                    bi += 1
                    s.op("act", lambda e, qb=qb, ps=ps, n=n: e.activation(out=qb, in_=ps[:, 0:n], func=AF.Copy), r=[Lp], w=[Lqb])
                    pr, Lpr = self.next_ps()
                    KV = int(os.environ.get("KVAR", "0"))
                    if KV == 0:
                        s.op("pe", lambda e, pr=pr, qb=qb, n=n: e.matmul(pr[:, 0:n], lhsT=self.rotP[:], rhs=qb, start=True, stop=True),
                             r=[Lqb, self.Lconst], w=[Lpr], c=PEC(n))
                    elif KV == 1:
                        s.op("pe", lambda e, pr=pr, qb=qb, n=n, wv=wv: e.matmul(pr[:, 0:n], lhsT=wv[:, 0, 0, :], rhs=qb, start=True, stop=True),
                             r=[Lqb, Lw], w=[Lpr], c=PEC(n))
                    elif KV == 2:
                        s.op("pe", lambda e, pr=pr, t0=t0, n=n: e.matmul(pr[:, 0:n], lhsT=self.rotP[:], rhs=self.hT[:, 0, t0:t0 + n], start=True, stop=True),
                             r=[Lqb, self.Lconst], w=[Lpr], c=PEC(n))
                    elif KV == 3:
                        s.op("pe", lambda e, pr=pr, t0=t0, n=n: e.matmul(pr[:, 0:n], lhsT=self.identb[:], rhs=self.hT[:, 0, t0:t0 + n], start=True, stop=True),
                             r=[Lqb, self.Lconst], w=[Lpr], c=PEC(n))
                    if KSUB < 3:
                        continue
                    t1 = self.vf32(tmp_o[0], 512)[:, 0:n]
                    t2 = self.vf32(tmp_o[1], 512)[:, 0:n]
                    s.op("dve", lambda e, t1=t1, ps=ps, t0=t0, n=n, rope=rope: e.tensor_tensor(out=t1, in0=ps[:, 0:n], in1=rope[:, 0, 0:n], op=ALU.mult),
                         r=[Lp, Lrope, Lqb], w=[tmp_L[0]])
                    s.op("dve", lambda e, t2=t2, pr=pr, t0=t0, n=n, rope=rope: e.tensor_tensor(out=t2, in0=pr[:, 0:n], in1=rope[:, 1, 0:n], op=ALU.mult),
                         r=[Lpr, Lrope], w=[tmp_L[1]])
                    if KSUB < 4:
                        continue
                    s.op("dve", lambda e, dstT=dstT, t1=t1, t2=t2, t0=t0, n=n: e.tensor_tensor(out=dstT[:, t0:t0 + n], in0=t1, in1=t2, op=ALU.add),
                         r=[tmp_L[0], tmp_L[1]], w=[dL[t]])
                    if t == 4 and ty == 0 and KSUB >= 5:
                        s.op("dve", lambda e, t1=t1, t2=t2: e.tensor_tensor(out=q32s, in0=t1, in1=t2, op=ALU.add),
                             r=[tmp_L[0], tmp_L[1]], w=[q32_L[0]])
                if KSUB < 5:
                    continue
                ps, Lp = self.next_ps()
                for k in range(8):
                    s.op("pe", lambda e, ps=ps, k=k, t0=t0, n=n, wv=wv: e.matmul(
                        ps[:, 0:n], lhsT=wv[:, 3, k, :], rhs=self.hT[:, k, t0:t0 + n], start=(k == 0), stop=(k == 7)),
                        r=[Lw, self.LhT[k][t]], w=[Lp], c=PEC(n))
                s.op("act", lambda e, ps=ps, t0=t0, n=n: e.activation(out=gate[:, t0:t0 + n], in_=ps[:, 0:n], func=AF.Silu), r=[Lp], w=[gt_L[t]])
            if KDBG < 2:
                self.wrelease(1)
                continue
            for t in range(5):
                ps, Lp = self.next_ps()
                nn = 4 if t < 4 else 1
                for j in range(nn):
                    tt = t * 4 + j
                    tok0 = tt * 128
                    nt = 128 if t < 4 else 64
                    for k in range(8):
                        s.op("pe", lambda e, ps=ps, j=j, k=k, tok0=tok0, nt=nt, wv=wv: e.matmul(
                            ps[0:nt, j * 128:(j + 1) * 128], lhsT=self.hT[:, k, tok0:tok0 + nt], rhs=wv[:, 2, k, :],
                            start=(k == 0), stop=(k == 7)), r=[Lw, self.LhT[k][t]], w=[Lp])
                if t < 4:
                    s.op("act", lambda e, ps=ps, t=t: e.activation(out=vtok[:, 4 * t:4 * t + 4, :], in_=ps[:].rearrange("p (n d) -> p n d", n=4),
                                                                   func=AF.Copy), r=[Lp], w=[v_L[t]])
                else:
                    s.op("act", lambda e, ps=ps: e.activation(out=vtok[0:64, 16, :], in_=ps[0:64, 0:128], func=AF.Copy), r=[Lp], w=[v_L[4]])
            if KDBG < 3:
                self.wrelease(1)
                continue
            for t in range(5):
                ps, Lp = self.next_ps()
                psb = ps[:].bitcast(BF16)
                if t < 4:
                    for j in range(4):
                        n_ = 4 * t + j
                        s.op("pe", lambda e, psb=psb, j=j, n_=n_: e.transpose(psb[:, j * 128:(j + 1) * 128], kT[:, n_ * 128:(n_ + 1) * 128], self.identb[:]),
                             r=[k_L[t], self.Lconst], w=[Lp])
                    s.op("dve", lambda e, psb=psb, t=t, hh=hh: e.tensor_scalar(
                        out=kdtok[:, 4 * t:4 * t + 4, :], in0=psb[:, 0:512].rearrange("p (n d) -> p n d", n=4),
                        scalar1=tabs[:, TC_KDEC + hh:TC_KDEC + hh + 1], scalar2=None, op0=ALU.mult), r=[Lp, self.Lconst], w=[kd_L[t]])
                else:
                    s.op("pe", lambda e, psb=psb: e.transpose(psb[0:64, 0:128], kT[:, SEQ:SEQ + 64], self.identb[:]),
                         r=[k_L[4], self.Lconst], w=[Lp])
                    s.op("dve", lambda e, psb=psb, hh=hh: e.tensor_scalar(
                        out=kdtok[0:64, 16, :], in0=psb[0:64, 0:128], scalar1=tabs[0:64, TC_KDECS + hh:TC_KDECS + hh + 1],
                        scalar2=None, op0=ALU.mult), r=[Lp, self.Lconst], w=[kd_L[4]])
            self.wrelease(1)
            if KDBG < 4:
                continue
            cd = float(self.gam[hh] ** 128)
            for t in range(4):
                ps, Lp = self.next_ps()
                for j in range(4):
                    n_ = 4 * t + j
                    s.op("pe", lambda e, ps=ps, j=j, n_=n_: e.matmul(ps[:, j * 128:(j + 1) * 128], lhsT=kdtok[:, n_, :], rhs=vtok[:, n_, :],
                                                                    start=True, stop=True), r=[kd_L[t], v_L[t]], w=[Lp])
                for j in range(4):
                    n_ = 4 * t + j
                    if n_ == 0:
                        s.op("dve", lambda e, ps=ps: e.tensor_copy(out=S32, in_=ps[:, 0:128]), r=[Lp], w=[s32_L[0]])
                    else:
                        s.op("dve", lambda e, ps=ps, j=j, cd=cd: e.scalar_tensor_tensor(out=S32, in0=S32, scalar=cd, in1=ps[:, j * 128:(j + 1) * 128],
                                                                                op0=ALU.mult, op1=ALU.add), r=[Lp], w=[s32_L[0]])
                    if n_ < 15:
                        s.op("act", lambda e, n_=n_: e.activation(out=Sbf[:, n_ + 1, :], in_=S32, func=AF.Copy), r=[s32_L[0]], w=[sb_L[n_ + 1]])
            s.dma("sp", lambda e, h, hh=hh: e.dma_start(out=d["nrp"][l, hh], in_=S32).then_inc(h, 16), dso, 1, r=[s32_L[0]], w=[LT()])
            if KDBG < 5:
                continue
            for t, (t0, n) in enumerate(TT):
                if KDBG < 6 and t == 4:
                    continue
                sT = self.vbf(st_o[t % 2], 512)
                LsT = st_L[t % 2]
                po, Lpo = self.next_ps()
                if t < 4:
                    ps, Lp = self.next_ps()
                    for j in range(4):
                        n_ = 4 * t + j
                        s.op("pe", lambda e, ps=ps, j=j, n_=n_: e.matmul(ps[:, j * 128:(j + 1) * 128], lhsT=kT[:, n_ * 128:(n_ + 1) * 128],
                                                                        rhs=qT[:, n_ * 128:(n_ + 1) * 128], start=True, stop=True),
                             r=[k_L[t], q_L[t]], w=[Lp])
                    s.op("dve", lambda e, ps=ps, sT=sT, hh=hh: e.tensor_tensor(
                        out=sT.rearrange("p (n i) -> p n i", n=4), in0=ps[:].rearrange("p (n i) -> p n i", n=4),
                        in1=tabs[:, TC_MASKP + hh * 128:TC_MASKP + (hh + 1) * 128].unsqueeze(1).to_broadcast([128, 4, 128]), op=ALU.mult),
                        r=[Lp, self.Lconst], w=[LsT])
                    for j in range(4):
                        n_ = 4 * t + j
                        s.op("pe", lambda e, po=po, j=j, n_=n_, sT=sT: e.matmul(po[:, j * 128:(j + 1) * 128], lhsT=vtok[:, n_, :],
                                                                               rhs=sT[:, j * 128:(j + 1) * 128], start=True, stop=(n_ == 0)),
                             r=[v_L[t], LsT], w=[Lpo])
                        if n_ > 0:
                            s.op("pe", lambda e, po=po, j=j, n_=n_: e.matmul(po[:, j * 128:(j + 1) * 128], lhsT=Sbf[:, n_, :],
                                                                            rhs=qT[:, n_ * 128:(n_ + 1) * 128], start=False, stop=True),
                                 r=[sb_L[n_], q_L[t]], w=[Lpo])
                    osb = self.vf32(tmp_o[2], 512)
                    s.op("dve", lambda e, po=po, osb=osb, hh=hh: e.tensor_tensor(
                        out=osb.rearrange("p (n i) -> p n i", n=4), in0=po[:].rearrange("p (n i) -> p n i", n=4),
                        in1=tabs[:, TC_QDEC + hh * 128:TC_QDEC + (hh + 1) * 128].unsqueeze(1).to_broadcast([128, 4, 128]), op=ALU.mult),
                        r=[Lpo, self.Lconst], w=[tmp_L[2]])
                else:
                    ps, Lp = self.next_ps()
                    s.op("pe", lambda e, ps=ps: e.matmul(ps[0:64, 0:64], lhsT=kT[:, SEQ:SEQ + 64], rhs=qT[:, SEQ:SEQ + 64], start=True, stop=True),
                         r=[k_L[4], q_L[4]], w=[Lp])
                    s.op("dve", lambda e, ps=ps, sT=sT, hh=hh: e.tensor_tensor(out=sT[0:64, 0:64], in0=ps[0:64, 0:64],
                                                                             in1=tabs[0:64, TC_MASKS + hh * 64:TC_MASKS + (hh + 1) * 64], op=ALU.mult),
                         r=[Lp, self.Lconst], w=[LsT])
                    s.op("pe", lambda e, po=po, sT=sT: e.matmul(po[:, 0:64], lhsT=vtok[0:64, 16, :], rhs=sT[0:64, 0:64], start=True, stop=True),
                         r=[v_L[4], LsT], w=[Lpo])
                    pi, Lpi = self.next_ps()
                    for b in range(16):
                        s.op("pe", lambda e, pi=pi, b=b: e.matmul(pi[:, 4 * b:4 * b + 4], lhsT=S0[:, b, :], rhs=q32s[:, 4 * b:4 * b + 4], start=True, stop=True),
                             r=[s0_L[0], q32_L[0]], w=[Lpi])
                    oi = self.vf32(tmp_o[1], 512)[:, 0:64]
                    s.op("act", lambda e, pi=pi, oi=oi: e.activation(out=oi, in_=pi[:, 0:64], func=AF.Copy), r=[Lpi], w=[tmp_L[1]])
                    osb = self.vf32(tmp_o[2], 512)
                    s.op("dve", lambda e, po=po, oi=oi, osb=osb: e.tensor_tensor(out=osb[:, 0:64], in0=po[:, 0:64], in1=oi, op=ALU.add),
                         r=[Lpo, tmp_L[1]], w=[tmp_L[2]])
                    s.op("dve", lambda e, osb=osb, hh=hh: e.tensor_tensor(out=osb[:, 0:64], in0=osb[:, 0:64],
                                                                        in1=tabs[:, TC_QDECS + hh * 64:TC_QDECS + (hh + 1) * 64], op=ALU.mult),
                         r=[self.Lconst], w=[tmp_L[2]])
                    self.phase_reset(keep=keep)
                    km_o, km_L = self.alloc(16 * 128 * 2, 1, "kmask")
                    kmask = self.vbf(km_o, 16 * 128).rearrange("p (b d) -> p b d", b=16)
                    s.op("dve", lambda e: e.tensor_tensor(
                        out=kmask[0:64], in0=kdtok[0:64, 16, :].unsqueeze(1).to_broadcast([64, 16, 128]),
                        in1=tabs[0:64, TC_ONEH:TC_ONEH + 16].unsqueeze(2).to_broadcast([64, 16, 128]), op=ALU.mult),
                        r=[kd_L[4], self.Lconst], w=[km_L[0]])
                    cd4 = float(self.gam[hh] ** 4)
                    for bg in range(4):
                        pk, Lpk = self.next_ps()
                        for j in range(4):
                            b = bg * 4 + j
                            s.op("pe", lambda e, pk=pk, j=j, b=b: e.matmul(pk[:, j * 128:(j + 1) * 128], lhsT=kmask[0:64, b, :], rhs=vtok[0:64, 16, :],
                                                                          start=True, stop=True), r=[km_L[0], v_L[4]], w=[Lpk])
                        s.op("dve", lambda e, pk=pk, bg=bg, cd4=cd4: e.scalar_tensor_tensor(
                            out=S0[:, 4 * bg:4 * bg + 4, :], in0=S0[:, 4 * bg:4 * bg + 4, :], scalar=cd4,
                            in1=pk[:].rearrange("p (b d) -> p b d", b=4), op0=ALU.mult, op1=ALU.add), r=[Lpk], w=[s0_L[0]])
                    def st_s0(e, h, hh=hh):
                        for bg in range(4):
                            e.dma_start(out=d["nrs"][l][4 * bg:4 * bg + 4, hh].rearrange("b k v -> k b v"), in_=S0[:, 4 * bg:4 * bg + 4, :]).then_inc(h, 16)
                    s.dma("sp", st_s0, dso_s, 4, r=[s0_L[0]], w=[LT()])
                osbn = self.vf32(tmp_o[2], 512)[:, 0:n]
                sq = self.vf32(tmp_o[0], 512)[:, 0:n]
                s.op("act", lambda e, sq=sq, osbn=osbn: e.activation(out=sq, in_=osbn, func=AF.Square), r=[tmp_L[2]], w=[tmp_L[0]])
                pm, Lm = self.next_ps()
                pe2, Le = self.next_ps()
                s.op("pe", lambda e, pm=pm, osbn=osbn, n=n: e.matmul(pm[:, 0:n], lhsT=self.ones_gn[:], rhs=osbn, start=True, stop=True),
                     r=[tmp_L[2], self.Lconst], w=[Lm], c=PEC(n))
                s.op("pe", lambda e, pe2=pe2, sq=sq, n=n: e.matmul(pe2[:, 0:n], lhsT=self.ones_gn[:], rhs=sq, start=True, stop=True),
                     r=[tmp_L[0], self.Lconst], w=[Le], c=PEC(n))
                mean = self.vf32(tmp_o[3], 512)[:, 0:n]
                rstd = self.vf32(tmp_o[4], 512)[:, 0:n]
                self.norm_stats(pm, Lm, pe2, Le, mean, tmp_L[3], rstd, tmp_L[4], n, GN_EPS)
                s.op("dve", lambda e, osbn=osbn, mean=mean: e.tensor_tensor(out=osbn, in0=osbn, in1=mean, op=ALU.subtract), r=[tmp_L[3]], w=[tmp_L[2]])
                s.op("dve", lambda e, osbn=osbn, rstd=rstd: e.tensor_tensor(out=osbn, in0=osbn, in1=rstd, op=ALU.mult), r=[tmp_L[4]], w=[tmp_L[2]])
                s.op("dve", lambda e, osbn=osbn, hh=hh, t0=t0, n=n: e.scalar_tensor_tensor(
                    out=ro[:, hh, t0:t0 + n], in0=osbn, scalar=self.pc("gn_g", l, hh), in1=gate[:, t0:t0 + n], op0=ALU.mult, op1=ALU.mult),
                    r=[tmp_L[2], gt_L[t], self.Lconst], w=[ro_L[hh * 5 + t]])
        if KDBG < 6:
            return
        self.proj_acc_rows("w_out", l, 512, lambda k, t: (ro[:, k, TT[t][0]:TT[t][0] + TT[t][1]], ro_L[k * 5 + t]), 1.0)

    def crossattn(self, l):
        s, d = self.s, self.d
        self.rmsnorm("g_ca", l)
        self.phase_reset()
        ocs_o, ocs_L = self.alloc(8 * 64 * 2, 1, "ocas")
        ocas = self.vbf(ocs_o, 8 * 64).rearrange("p (c t) -> p c t", c=8)
        qca_o, qca_L = self.alloc(8 * SEQ * 2, 32, "qca")
        qca = self.vbf(qca_o, 8 * SEQ).rearrange("p (c t) -> p c t", c=8)
        keep = self.aoff
        qcs_o, qcs_L = self.alloc(8 * 64 * 2, 8, "qcas")
        qcas = self.vbf(qcs_o, 8 * 64).rearrange("p (c t) -> p c t", c=8)
        NKV = 3
        kb_o, kb_L, vb_o, vb_L = [], [], [], []
        for i in range(NKV):
            o, L = self.alloc(4096, 1, "kb"); kb_o.append(o); kb_L.append(L[0])
            o, L = self.alloc(4096, 1, "vb"); vb_o.append(o); vb_L.append(L[0])
        kTb_o, kTb_L = [], []
        for i in range(2):
            o, L = self.alloc(4096, 1, "kTb"); kTb_o.append(o); kTb_L.append(L[0])
        pT_o, pT_L = [], []
        for i in range(2):
            o, L = self.alloc(2048, 1, "pT"); pT_o.append(o); pT_L.append(L[0])
        rd_o, rd_L = [], []
        for i in range(2):
            o, L = self.alloc(2048, 1, "rden"); rd_o.append(o); rd_L.append(L[0])
        dmem = s.dsem("mem%d" % l)
        dkv = [s.dsem("kv%d_%d" % (l, i)) for i in range(NKV)]
        dmo = [s.dsem("mo%d_%d" % (l, i)) for i in range(2)]
        wq = [self.wblock(("A", "w_cq", l, 0)), self.wblock(("A", "w_cq", l, 512))]

        def qgroup(half, t, ql, dstq, dstL, col0):
            ring, Lw = wq[half]
            wv = ring[:].rearrange("p (k n) -> p k n", k=8)
            t0, n = TT[t]
            qc = half * 4 + ql
            ps, Lp = self.next_ps()
            for k in range(8):
                s.op("pe", lambda e, ps=ps, wv=wv, k=k, ql=ql, t0=t0, n=n: e.matmul(
                    ps[:, 0:n], lhsT=wv[:, k, ql * 128:(ql + 1) * 128], rhs=self.hT[:, k, t0:t0 + n], start=(k == 0), stop=(k == 7)),
                    r=[Lw, self.LhT[k][t]], w=[Lp], c=PEC(n))
            s.op("act", lambda e, ps=ps, qc=qc, t0=t0, n=n: e.activation(out=dstq[:, qc, t0 - col0:t0 - col0 + n], in_=ps[:, 0:n], func=AF.Copy),
                 r=[Lp], w=[dstL(qc, t)])

        def qproj(tiles, dstq, dstL, col0):
            for half in range(2):
                for t in tiles:
                    for ql in range(4):
                        qgroup(half, t, ql, dstq, dstL, col0)
        pq = [(half, t, ql) for half in range(2) for t in range(4) for ql in range(4)]
        qproj([4], qcas, lambda qc, t: qcs_L[qc], SEQ)

        def load_kv(b):
            sl = b % NKV
            kb = self.vbf(kb_o[sl], 2048).rearrange("p (c d) -> p c d", c=2)
            vb = self.vbf(vb_o[sl], 2048).rearrange("p (c d) -> p c d", c=2)

            def fn(e, h):
                e.dma_start(out=kb, in_=d["cmk"][l, b].rearrange("(c p) d -> p c d", p=128)).then_inc(h, 16)
                e.dma_start(out=vb, in_=d["cmv"][l, b].rearrange("(c p) d -> p c d", p=128)).then_inc(h, 16)
            s.dma("pool", fn, dkv[sl], 2, w=[kb_L[sl], vb_L[sl]], c=9.0)
        for b in range(NKV - 1):
            load_kv(b)
        for b in range(16):
            if b + NKV - 1 < 16:
                load_kv(b + NKV - 1)
            sl = b % NKV
            kTb = self.vbf(kTb_o[b % 2], 2048).rearrange("p (c m) -> p c m", c=8)
            LkTb = kTb_L[b % 2]
            kb = self.vbf(kb_o[sl], 2048).rearrange("p (c d) -> p c d", c=2)
            vb = self.vbf(vb_o[sl], 2048).rearrange("p (c d) -> p c d", c=2)
            for mc in range(2):
                ps, Lp = self.next_ps()
                psb = ps[:].bitcast(BF16)
                for c in range(8):
                    s.op("pe", lambda e, psb=psb, mc=mc, c=c, kb=kb: e.transpose(psb[:, c * 128:(c + 1) * 128], kb[:, mc, c * 128:(c + 1) * 128], self.identb[:]),
                         r=[kb_L[sl], self.Lconst], w=[Lp])
                eng = "act" if mc == 0 else "dve"
                if eng == "act":
                    s.op("act", lambda e, psb=psb, mc=mc, kTb=kTb: e.activation(out=kTb[:, :, mc * 128:(mc + 1) * 128], in_=psb.rearrange("p (c m) -> p c m", c=8), func=AF.Copy),
                         r=[Lp], w=[LkTb])
                else:
                    s.op("dve", lambda e, psb=psb, mc=mc, kTb=kTb: e.tensor_copy(out=kTb[:, :, mc * 128:(mc + 1) * 128], in_=psb.rearrange("p (c m) -> p c m", c=8)),
                         r=[Lp], w=[LkTb])
            ps, Lp = self.next_ps()
            for hh in range(4):
                for mc in range(2):
                    col = (mc * 4 + hh) * 4
                    for hf in range(2):
                        s.op("pe", lambda e, ps=ps, col=col, hh=hh, mc=mc, hf=hf, b=b, kTb=kTb: e.matmul(
                            ps[:, col:col + 4], lhsT=kTb[:, 2 * hh + hf, mc * 128:(mc + 1) * 128], rhs=qcas[:, 2 * hh + hf, 4 * b:4 * b + 4],
                            start=(hf == 0), stop=(hf == 1)), r=[LkTb, qcs_L[2 * hh + hf]], w=[Lp], c=0.06)
            pT = self.vbf(pT_o[b % 2], 1024)[:, 0:32]
            LpT = pT_L[b % 2]
            s.op("act", lambda e, pT=pT, ps=ps: e.activation(out=pT, in_=ps[:, 0:32], func=AF.Exp, scale=1.0 / 16), r=[Lp], w=[LpT])
            pden, Lpd = self.next_ps()
            for mc in range(2):
                s.op("pe", lambda e, pden=pden, pT=pT, mc=mc: e.matmul(pden[:, 0:16], lhsT=self.ones_b[:], rhs=pT[:, mc * 16:(mc + 1) * 16],
                                                                       start=(mc == 0), stop=(mc == 1)), r=[LpT, self.Lconst], w=[Lpd])
            rden = self.vf32(rd_o[b % 2], 512)[:, 0:16]
            Lrd = rd_L[b % 2]
            s.op("dve", lambda e, rden=rden, pden=pden: e.reciprocal(out=rden, in_=pden[:, 0:16]), r=[Lpd], w=[Lrd])
            po, Lpo = self.next_ps()
            for hh in range(4):
                for dh in range(2):
                    col = (hh * 2 + dh) * 4
                    for mc in range(2):
                        s.op("pe", lambda e, po=po, col=col, hh=hh, dh=dh, mc=mc, vb=vb, pT=pT: e.matmul(
                            po[:, col:col + 4], lhsT=vb[:, mc, hh * 256 + dh * 128:hh * 256 + (dh + 1) * 128], rhs=pT[:, (mc * 4 + hh) * 4:(mc * 4 + hh) * 4 + 4],
                            start=(mc == 0), stop=(mc == 1)), r=[vb_L[sl], LpT], w=[Lpo])
            s.op("dve", lambda e, po=po, rden=rden, b=b: e.tensor_tensor(
                out=ocas[:, :, 4 * b:4 * b + 4].rearrange("p (h d) l -> p h d l", h=4),
                in0=po[:, 0:32].rearrange("p (h d l) -> p h d l", h=4, d=2),
                in1=rden.rearrange("p (h l) -> p h l", h=4).unsqueeze(2).to_broadcast([128, 4, 2, 4]), op=ALU.mult),
                r=[Lpo, Lrd], w=[ocs_L[0]])
            for _ in range(2):
                if pq:
                    g_ = pq.pop(0)
                    qgroup(g_[0], g_[1], g_[2], qca, lambda qc, t: qca_L[qc * 4 + t], 0)
        while pq:
            g_ = pq.pop(0)
            qgroup(g_[0], g_[1], g_[2], qca, lambda qc, t: qca_L[qc * 4 + t], 0)
        self.wrelease(2)
        self.phase_reset(keep=keep)
        kTm_o, kTm_L = self.alloc(8 * 256 * 2, 1, "kTm")
        kTm = self.vbf(kTm_o, 8 * 256).rearrange("p (c m) -> p c m", c=8)
        vm_o, vm_L = self.alloc(2 * 1024 * 2, 1, "vm")
        vm = self.vbf(vm_o, 2048).rearrange("p (c d) -> p c d", c=2)
        mnT_o, mnT_L = self.alloc(8 * 256 * 2, 1, "mnT")
        mnT = self.vbf(mnT_o, 8 * 256).rearrange("p (c m) -> p c m", c=8)
        mst_o, mst_L = self.alloc(2 * 1024 * 4, 1, "memst")
        mst = self.vf32(mst_o, 2048).rearrange("p (c d) -> p c d", c=2)
        sm_o, sm_L = self.alloc(1024, 1, "small")
        ost_o, ost_L = [], []
        for i in range(2):
            o, L = self.alloc(2048, 1, "ost"); ost_o.append(o); ost_L.append(L[0])
        pT_o, pT_L = [], []
        for i in range(2):
            o, L = self.alloc(2048, 1, "pT2"); pT_o.append(o); pT_L.append(L[0])
        rd_o, rd_L = [], []
        for i in range(2):
            o, L = self.alloc(2048, 1, "rden2"); rd_o.append(o); rd_L.append(L[0])
        s.dma("sp", lambda e, h: e.dma_start(out=mst, in_=d["memp"].rearrange("(c p) d -> p c d", p=128)).then_inc(h, 16), dmem, 1, w=[mst_L[0]])
        small = self.vf32(sm_o, 256)
        junk = self.vf32(ost_o[0], 512)
        for mc in range(2):
            for hf in range(2):
                s.op("act", lambda e, mc=mc, hf=hf: e.activation(out=junk, in_=mst[:, mc, hf * 512:(hf + 1) * 512], func=AF.Square,
                                                                 accum_out=small[:, mc * 2 + hf:mc * 2 + hf + 1]), r=[mst_L[0]], w=[ost_L[0], sm_L[0]])
        s.op("dve", lambda e: e.tensor_tensor(out=small[:, 4:6], in0=small[:, 0:4].rearrange("p (a b) -> p a b", b=2)[:, :, 0],
                                              in1=small[:, 0:4].rearrange("p (a b) -> p a b", b=2)[:, :, 1], op=ALU.add), r=[sm_L[0]], w=[sm_L[0]])
        s.op("dve", lambda e: e.tensor_scalar(out=small[:, 4:6], in0=small[:, 4:6], scalar1=1.0 / 1024, scalar2=None, op0=ALU.mult),
             r=[sm_L[0]], w=[sm_L[0]])
        s.op("act", lambda e: e.activation(out=small[:, 4:6], in_=small[:, 4:6], func=AF.Sqrt, bias=EPS, scale=1.0), r=[sm_L[0]], w=[sm_L[0]])
        s.op("dve", lambda e: e.reciprocal(out=small[:, 4:6], in_=small[:, 4:6]), r=[sm_L[0]], w=[sm_L[0]])
        for mc in range(2):
            s.op("dve", lambda e, mc=mc: e.tensor_scalar(out=mst[:, mc, :], in0=mst[:, mc, :], scalar1=small[:, 4 + mc:5 + mc], scalar2=None, op0=ALU.mult),
                 r=[sm_L[0]], w=[mst_L[0]])
        for c in range(8):
            ps, Lp = self.next_ps()
            for mc in range(2):
                s.op("pe", lambda e, ps=ps, mc=mc, c=c: e.transpose(ps[:, mc * 128:(mc + 1) * 128], mst[:, mc, c * 128:(c + 1) * 128], self.identf[:]),
                     r=[mst_L[0], self.Lconst], w=[Lp])
            s.op("dve", lambda e, ps=ps, c=c: e.tensor_scalar(out=mnT[:, c, :], in0=ps[:, 0:256], scalar1=self.pc("g_mem", l, c), scalar2=None, op0=ALU.mult),
                 r=[Lp, self.Lconst], w=[mnT_L[0]])
        wk = [self.wblock(("A", "w_ck", l, 0)), self.wblock(("A", "w_ck", l, 512))]
        for half in range(2):
            ring, Lw = wk[half]
            wv = ring[:].rearrange("p (k n) -> p k n", k=8)
            for ql in range(4):
                c = half * 4 + ql
                ps, Lp = self.next_ps()
                for k in range(8):
                    s.op("pe", lambda e, ps=ps, wv=wv, k=k, ql=ql: e.matmul(ps[:, 0:256], lhsT=wv[:, k, ql * 128:(ql + 1) * 128], rhs=mnT[:, k, :],
                                                                           start=(k == 0), stop=(k == 7)), r=[Lw, mnT_L[0]], w=[Lp])
                s.op("act", lambda e, ps=ps, c=c: e.activation(out=kTm[:, c, :], in_=ps[:, 0:256], func=AF.Copy), r=[Lp], w=[kTm_L[0]])
            for mc in range(2):
                ps, Lp = self.next_ps()
                for k in range(8):
                    s.op("pe", lambda e, ps=ps, wv=wv, k=k, mc=mc: e.matmul(ps[:], lhsT=mnT[:, k, mc * 128:(mc + 1) * 128], rhs=wv[:, k, :],
                                                                           start=(k == 0), stop=(k == 7)), r=[Lw, mnT_L[0]], w=[Lp])
                oi = (half * 2 + mc) % 2
                ost = self.vf32(ost_o[oi], 512)
                s.op("act", lambda e, ps=ps, ost=ost: e.activation(out=ost, in_=ps[:], func=AF.Copy), r=[Lp], w=[ost_L[oi]])
                s.dma("sp", lambda e, h, ost=ost, mc=mc, half=half: e.dma_start(out=d["nmk"][l, mc * 128:(mc + 1) * 128, half * 512:(half + 1) * 512], in_=ost).then_inc(h, 16),
                      dmo[oi], 1, r=[ost_L[oi]], w=[LT()])
        self.wrelease(2)
        wvv = [self.wblock(("A", "w_cv", l, 0)), self.wblock(("A", "w_cv", l, 512))]
        for half in range(2):
            ring, Lw = wvv[half]
            wv = ring[:].rearrange("p (k n) -> p k n", k=8)
            for mc in range(2):
                ps, Lp = self.next_ps()
                for k in range(8):
                    s.op("pe", lambda e, ps=ps, wv=wv, k=k, mc=mc: e.matmul(ps[:], lhsT=mnT[:, k, mc * 128:(mc + 1) * 128], rhs=wv[:, k, :],
                                                                           start=(k == 0), stop=(k == 7)), r=[Lw, mnT_L[0]], w=[Lp])
                oi = (half * 2 + mc) % 2
                ost = self.vf32(ost_o[oi], 512)
                s.op("act", lambda e, ps=ps, ost=ost: e.activation(out=ost, in_=ps[:], func=AF.Copy), r=[Lp], w=[ost_L[oi]])
                s.op("dve", lambda e, ost=ost, mc=mc, half=half: e.tensor_copy(out=vm[:, mc, half * 512:(half + 1) * 512], in_=ost), r=[ost_L[oi]], w=[vm_L[0]])
                s.dma("sp", lambda e, h, ost=ost, mc=mc, half=half: e.dma_start(out=d["nmv"][l, mc * 128:(mc + 1) * 128, half * 512:(half + 1) * 512], in_=ost).then_inc(h, 16),
                      dmo[oi], 1, r=[ost_L[oi]], w=[LT()])
        self.wrelease(2)
        pi_ = 0
        for hh in range(4):
            for t in range(4):
                t0, n = TT[t]
                pT = self.vbf(pT_o[pi_ % 2], 1024).rearrange("p (c t) -> p c t", c=2)
                LpT = pT_L[pi_ % 2]
                rden = self.vf32(rd_o[pi_ % 2], 512)
                Lrd = rd_L[pi_ % 2]
                pi_ += 1
                for mc in range(2):
                    ps, Lp = self.next_ps()
                    for hf in range(2):
                        s.op("pe", lambda e, ps=ps, hh=hh, mc=mc, hf=hf, t0=t0: e.matmul(
                            ps[:], lhsT=kTm[:, 2 * hh + hf, mc * 128:(mc + 1) * 128], rhs=qca[:, 2 * hh + hf, t0:t0 + 512],
                            start=(hf == 0), stop=(hf == 1)), r=[kTm_L[0], qca_L[(2 * hh + hf) * 4 + t]], w=[Lp])
                    s.op("act", lambda e, ps=ps, pT=pT, mc=mc: e.activation(out=pT[:, mc, :], in_=ps[:], func=AF.Exp, scale=1.0 / 16), r=[Lp], w=[LpT])
                pden, Lpd = self.next_ps()
                for mc in range(2):
                    s.op("pe", lambda e, pden=pden, pT=pT, mc=mc: e.matmul(pden[:], lhsT=self.ones_b[:], rhs=pT[:, mc, :], start=(mc == 0), stop=(mc == 1)),
                         r=[LpT, self.Lconst], w=[Lpd])
                s.op("act", lambda e, rden=rden, pden=pden: e.activation(out=rden, in_=pden[:], func=AF.Ln), r=[Lpd], w=[Lrd])
                s.op("act", lambda e, rden=rden: e.activation(out=rden, in_=rden, func=AF.Exp, scale=-1.0), r=[Lrd], w=[Lrd])
                for dh in range(2):
                    po, Lpo = self.next_ps()
                    for mc in range(2):
                        s.op("pe", lambda e, po=po, hh=hh, dh=dh, mc=mc, pT=pT: e.matmul(
                            po[:], lhsT=vm[:, mc, hh * 256 + dh * 128:hh * 256 + (dh + 1) * 128], rhs=pT[:, mc, :], start=(mc == 0), stop=(mc == 1)),
                            r=[vm_L[0], LpT], w=[Lpo])
                    c = 2 * hh + dh
                    s.op("dve", lambda e, po=po, rden=rden, c=c, t0=t0: e.tensor_tensor(out=self.hT[:, c, t0:t0 + 512], in0=po[:], in1=rden, op=ALU.mult),
                         r=[Lpo, Lrd], w=[self.LhT[c][t]])
        def oin(k, t):
            if t < 4:
                return self.hT[:, k, TT[t][0]:TT[t][0] + 512], self.LhT[k][t]
            return ocas[:, k, :], ocs_L[0]
        self.proj_acc([("A", "w_co", l, 0), ("A", "w_co", l, 512)], oin, 8, 1.0)

    def final(self, raw=False):
        s, d = self.s, self.d
        self.phase_reset()
        NF = 4
        yT_o, yT_L = [], []
        for i in range(2):
            o, L = self.alloc(8 * 512 * 4, 4, "yT"); yT_o.append(o); yT_L.append(L)
        og_o, og_L = [], []
        for i in range(NF):
            o, L = self.alloc(4096, 1, "ostg"); og_o.append(o); og_L.append(L[0])
        sq_o, sq_L = [], []
        for i in range(3):
            o, L = self.alloc(1024, 1, "fsq"); sq_o.append(o); sq_L.append(L[0])
        rs_o, rs_L = [], []
        for i in range(2):
            o, L = self.alloc(2048, 1, "frstd"); rs_o.append(o); rs_L.append(L[0])
        dso = [s.dsem("yout%d" % i) for i in range(NF)]
        qi = 0
        oi = 0
        for t, (t0, n) in enumerate(TT):
            yT = self.vf32(yT_o[t % 2], 8 * 512).rearrange("p (c t) -> p c t", c=8)
            LyT4 = yT_L[t % 2]
            if not raw:
                ps, Lp = self.next_ps()
                for c in range(8):
                    sq = self.vbf(sq_o[qi % 3], 512)[:, 0:n]
                    Lq = sq_L[qi % 3]
                    qi += 1
                    s.op("act", lambda e, sq=sq, c=c, t0=t0, n=n: e.activation(out=sq, in_=self.xT[:, c, t0:t0 + n], func=AF.Square),
                         r=[self.LxT[c][t]], w=[Lq])
                    s.op("pe", lambda e, ps=ps, sq=sq, c=c, n=n: e.matmul(ps[:, 0:n], lhsT=self.ones_rms[:], rhs=sq, start=(c == 0), stop=(c == 7)),
                         r=[Lq, self.Lconst], w=[Lp], c=PEC(n))
                rstd = self.vf32(rs_o[t % 2], 512)[:, 0:n]
                Lr = rs_L[t % 2]
                s.op("act", lambda e, rstd=rstd, ps=ps, n=n: e.activation(out=rstd, in_=ps[:, 0:n], func=AF.Ln, bias=EPS, scale=1.0), r=[Lp], w=[Lr])
                s.op("act", lambda e, rstd=rstd: e.activation(out=rstd, in_=rstd, func=AF.Exp, scale=-0.5), r=[Lr], w=[Lr])
                for c in range(8):
                    s.op("dve", lambda e, c=c, t0=t0, n=n, rstd=rstd, yT=yT: e.scalar_tensor_tensor(
                        out=yT[:, c, 0:n], in0=self.xT[:, c, t0:t0 + n], scalar=self.pc("g_final", 0, c), in1=rstd,
                        op0=ALU.mult, op1=ALU.mult), r=[self.LxT[c][t], Lr, self.Lconst] + LyT4, w=LyT4)
            else:
                s.op("dve", lambda e, t0=t0, n=n, yT=yT: e.tensor_copy(out=yT[:, :, 0:n], in_=self.xT[:, :, t0:t0 + n]),
                     r=[self.LxT[c][t] for c in range(8)], w=LyT4)
            nsub = 4 if t < 4 else 1
            for j4 in range(nsub):
                nn = 128 if t < 4 else 64
                tok0 = t0 + j4 * 128
                og = self.vf32(og_o[oi % NF], 1024)
                Log = og_L[oi % NF]
                dsem = dso[oi % NF]
                oi += 1
                for half in range(2):
                    ps, Lp = self.next_ps()
                    for j in range(4):
                        c = half * 4 + j
                        s.op("pe", lambda e, ps=ps, j=j, c=c, yT=yT, nn=nn, j4=j4: e.transpose(
                            ps[0:nn, j * 128:(j + 1) * 128], yT[:, c, j4 * 128:j4 * 128 + nn], self.identf[:]),
                            r=[LyT4[j4], self.Lconst], w=[Lp], c=0.25)
                    if half == 0:
                        s.op("act", lambda e, ps=ps, og=og, nn=nn: e.activation(out=og[0:nn, 0:512], in_=ps[0:nn, :], func=AF.Copy), r=[Lp], w=[Log])
                    else:
                        s.op("dve", lambda e, ps=ps, og=og, nn=nn: e.tensor_copy(out=og[0:nn, 512:1024], in_=ps[0:nn, :]), r=[Lp], w=[Log])
                dst = d["yp"][tok0:tok0 + 128, :] if t < 4 else d["ys"]
                s.dma("sp", lambda e, h, og=og, dst=dst, nn=nn: e.dma_start(out=dst, in_=og[0:nn, :]).then_inc(h, 16), dsem, 1, r=[Log], w=[LT()])

    def emit_all(self):
        sa = self.stop_after
        self.load_consts()
        self.load_x()
        step = 0

        def done():
            nonlocal step
            step += 1
            return sa is not None and step > sa
        for l in range(2):
            if done(): break
            self.rmsnorm("g_ffn1", l); self.ffn(l, "a")
            if done(): break
            self.rmsnorm("g_mix", l); self.conv(l)
            if done(): break
            self.retention(l)
            if done(): break
            self.crossattn(l)
            if done(): break
            self.rmsnorm("g_ffn2", l); self.ffn(l, "b")
        self.final(raw=(sa is not None))


def build_program(stop_after=None):
    wplan = None
    for _pass in range(2):
        nc = bass.Bass("TRN2", target_bir_lowering=False)
        s = Sched()
        with contextlib.ExitStack() as st:
            K = Kern(nc, s, st, wplan, stop_after)
            K.emit_all()
            if _pass == 0:
                wplan = K.wplan_out
                continue
            est = s.reorder() if REORDER else None
            cnt = s.emit(nc)
    return nc, (len(s.ops), cnt, est)


_CACHE = {}


def make_in_maps(inp):
    rope, tabs, mats, _ = const_tables()
    params = pack_params(inp)
    f = lambda a: np.ascontiguousarray(np.asarray(a, np.float32))
    shared = {"w1a": f(inp["w1_ffn1"]), "w3a": f(inp["w3_ffn1"]), "w2a": f(inp["w2_ffn1"]), "w_in": f(inp["w_in"]),
              "w_out": f(inp["w_out"]), "w_cq": f(inp["w_cq"]), "w_ck": f(inp["w_ck"]), "w_cv": f(inp["w_cv"]),
              "w_co": f(inp["w_co"]), "w1b": f(inp["w1_ffn2"]), "w3b": f(inp["w3_ffn2"]), "w2b": f(inp["w2_ffn2"]),
              "params": params, "rope": rope, "tabs": tabs, "mats": mats}
    maps = []
    for c in range(8):
        b0 = c * NB_S
        m = dict(shared)
        m["xp"] = f(inp["x_prompt"][c])
        m["xs"] = f(np.asarray(inp["x_sample"])[b0:b0 + NB_S].reshape(NS_TOK, D))
        m["sconv"] = f(np.asarray(inp["state_conv"])[:, b0:b0 + NB_S])
        m["sret"] = f(np.asarray(inp["state_ret"])[:, b0:b0 + NB_S])
        m["cmk"] = f(np.asarray(inp["cache_mem_k"])[:, b0:b0 + NB_S].reshape(2, NB_S, NMEM, D))
        m["cmv"] = f(np.asarray(inp["cache_mem_v"])[:, b0:b0 + NB_S].reshape(2, NB_S, NMEM, D))
        m["memp"] = f(inp["mem_prompt"][c])
        maps.append(m)
    return maps


def assemble(res):
    r = res
    y_prompt = np.stack([r[c]["yp"] for c in range(8)], 0)
    y_sample = np.concatenate([r[c]["ys"].reshape(NB_S, 4, D) for c in range(8)], 0)
    ncp = np.stack([r[c]["ncp"] for c in range(8)], 1)
    nrp = np.stack([r[c]["nrp"] for c in range(8)], 1)
    nmk = np.stack([r[c]["nmk"].reshape(2, NMEM, 4, 256) for c in range(8)], 1)
    nmv = np.stack([r[c]["nmv"].reshape(2, NMEM, 4, 256) for c in range(8)], 1)
    ncs = np.concatenate([r[c]["ncs"] for c in range(8)], 1)
    nrs = np.concatenate([r[c]["nrs"] for c in range(8)], 1)
    return tuple(np.ascontiguousarray(a, dtype=np.float32) for a in (y_prompt, y_sample, ncp, nrp, nmk, nmv, ncs, nrs))


def kernel(**inputs):
    if "nc" not in _CACHE:
        _CACHE["nc"] = build_program()[0]
    nc = _CACHE["nc"]
    maps = make_in_maps(inputs)
    res = run_bass_kernel_spmd(nc, maps, core_ids=list(range(8)))
    return assemble(res.results)
```

```python
import contextlib
import os
import numpy as np
import concourse.bass as bass
import concourse.mybir as mybir
from concourse.bass_utils import run_bass_kernel_spmd

F32 = mybir.dt.float32
BF16 = mybir.dt.bfloat16
AF = mybir.ActivationFunctionType
ALU = mybir.AluOpType

D = 1024
SEQ = 2048
NB_S = 16
NS_TOK = 64
NT = SEQ + NS_TOK
DFF = 4096
NMEM = 256
CONV_K = 31
EPS = 1e-6
GN_EPS = 1e-5
PAST_LEN = 16384
TT = [(0, 512), (512, 512), (1024, 512), (1536, 512), (2048, 64)]
ENGS = ("pe", "act", "dve", "pool", "sp")
NRING = 3
ENG_COST = {"pe": 0.22, "act": 0.5, "dve": 0.75, "pool": 1.0, "sp": 0.1}
DMA_COST = 3.0
SEM_LAT = 0.2
WINDOW = {"pe": 320, "act": 48, "dve": 48, "pool": 8, "sp": 12}
REORDER = int(os.environ.get("KREORDER", "1"))


def PEC(n):
    return max(0.056, n / 2400.0 + 0.004)
KDBG = int(os.environ.get("KDBG", "99"))
KSUB = int(os.environ.get("KSUB", "99"))
SAME_ENGINE_SYNC = True


class LT:
    __slots__ = ("name", "w", "rs")

    def __init__(self, name=""):
        self.name = name
        self.w = None
        self.rs = []


class DSem:
    __slots__ = ("name", "h", "val")

    def __init__(self, name):
        self.name = name
        self.h = None
        self.val = 0


class Op:
    __slots__ = ("idx", "eng", "fn", "deps", "is_dma", "dsem", "dval", "ndma", "sig", "cnt", "cost")


class Sched:
    def __init__(self):
        self.ops = []
        self.dsems = []

    def dsem(self, name):
        d = DSem(name)
        self.dsems.append(d)
        return d

    def _add(self, eng, fn, r, w, is_dma, dsem, ndma, c=None):
        op = Op()
        op.cost = c if c is not None else (DMA_COST if is_dma else ENG_COST[eng])
        op.idx = len(self.ops)
        op.eng = eng
        op.fn = fn
        op.is_dma = is_dma
        op.dsem = dsem
        op.ndma = ndma
        op.sig = False
        op.cnt = None
        deps = set()
        for t in r:
            if t.w is not None:
                deps.add(t.w)
        for t in w:
            if t.w is not None:
                deps.add(t.w)
            deps.update(t.rs)
        op.deps = sorted(deps)
        if is_dma:
            dsem.val += 16 * ndma
            op.dval = dsem.val
        else:
            op.dval = None
        for t in w:
            t.w = op.idx
            t.rs = []
        for t in r:
            if t.w != op.idx:
                t.rs.append(op.idx)
        self.ops.append(op)
        return op

    def op(self, eng, fn, r=(), w=(), c=None):
        return self._add(eng, fn, r, w, False, None, 0, c)

    def dma(self, eng, fn, dsem, ndma, r=(), w=(), c=None):
        return self._add(eng, fn, r, w, True, dsem, ndma, c)

    def reorder(self):
        ops = self.ops
        n = len(ops)
        q = {e: [o.idx for o in ops if o.eng == e] for e in ENGS}
        head = {e: 0 for e in ENGS}
        done = [False] * n
        fin = [0.0] * n
        free = {e: 0.0 for e in ENGS}
        order = []
        cand = {e: None for e in ENGS}
        dirty = set(ENGS)

        def scan(e):
            qe = q[e]
            i = head[e]
            while i < len(qe) and done[qe[i]]:
                i += 1
            head[e] = i
            best = None
            cnt = 0
            W = WINDOW[e]
            fe = free[e]
            while i < len(qe) and cnt < W:
                oi = qe[i]
                i += 1
                if done[oi]:
                    continue
                cnt += 1
                ready = 0.0
                ok = True
                for d_ in ops[oi].deps:
                    if not done[d_]:
                        ok = False
                        break
                    if fin[d_] > ready:
                        ready = fin[d_]
                if not ok:
                    continue
                st_ = ready if ready > fe else fe
                if best is None or st_ < best[0] - 1e-9:
                    best = (st_, oi)
                    if st_ <= fe + 1e-9:
                        break
            return best

        left = n
        while left:
            for e in ENGS:
                cand[e] = scan(e)
            pick = None
            for e in ENGS:
                c_ = cand[e]
                if c_ is not None and (pick is None or c_[0] < pick[0]):
                    pick = (c_[0], c_[1], e)
            assert pick is not None, "scheduler deadlock"
            st_, oi, e = pick
            o = ops[oi]
            done[oi] = True
            if o.is_dma:
                free[e] = st_ + 0.08
                fin[oi] = st_ + o.cost
            else:
                free[e] = st_ + o.cost
                fin[oi] = st_ + o.cost + SEM_LAT
            order.append(oi)
            left -= 1
        self.order = order
        return max(fin)

    def lt_inherit(self, new, olds):
        acc = set(new.rs)
        for o in olds:
            if o.w is not None:
                acc.add(o.w)
            acc.update(o.rs)
        new.rs = sorted(acc)

    def emit(self, nc):
        ops = self.ops
        seq = [ops[i] for i in self.order] if getattr(self, "order", None) else ops
        per_eng = {e: [o for o in seq if o.eng == e] for e in ENGS}
        pos = {}
        for e in ENGS:
            for i, o in enumerate(per_eng[e]):
                pos[o.idx] = i
        need = {}
        for o in seq:
            best = {}
            dmas = {}
            for di in o.deps:
                p = ops[di]
                if p.is_dma:
                    k = id(p.dsem)
                    if k not in dmas or dmas[k].dval < p.dval:
                        dmas[k] = p
                    continue
                if p.eng == o.eng and (not SAME_ENGINE_SYNC or p.eng == "pe"):
                    continue
                if p.eng not in best or pos[best[p.eng].idx] < pos[p.idx]:
                    best[p.eng] = p
            for p in best.values():
                p.sig = True
            need[o.idx] = (list(best.values()), list(dmas.values()))
        cnt = {e: 0 for e in ENGS}
        for o in seq:
            if not o.is_dma and o.sig:
                cnt[o.eng] += 1
                o.cnt = cnt[o.eng]
        with contextlib.ExitStack() as st:
            esem = {e: st.enter_context(nc.semaphore("es_" + e)) for e in ENGS}
            for d in self.dsems:
                d.h = st.enter_context(nc.semaphore("ds_" + d.name))
            block = st.enter_context(nc.Block())

            def run(e, eng):
                waited = {}
                for o in per_eng[e]:
                    cps, dms = need[o.idx]
                    for p in cps:
                        key = ("e", p.eng)
                        if waited.get(key, 0) >= p.cnt:
                            continue
                        waited[key] = p.cnt
                        eng.wait_ge(esem[p.eng], p.cnt)
                    for p in dms:
                        key = ("d", id(p.dsem))
                        if waited.get(key, 0) >= p.dval:
                            continue
                        waited[key] = p.dval
                        eng.wait_ge(p.dsem.h, p.dval)
                    if o.is_dma:
                        o.fn(eng, o.dsem.h)
                    else:
                        ins = o.fn(eng)
                        if o.sig:
                            ins.then_inc(esem[e], 1)
                if e == "sp":
                    for d in self.dsems:
                        if d.val > 0:
                            eng.wait_ge(d.h, d.val)

            block.tensor(lambda eng: run("pe", eng))
            block.scalar(lambda eng: run("act", eng))
            block.vector(lambda eng: run("dve", eng))
            block.gpsimd(lambda eng: run("pool", eng))
            block.sync(lambda eng: run("sp", eng))
        return cnt


PCOL = {}
_c = 0
for _l in range(2):
    for _n, _w in (("g_ffn1", 8), ("g_mix", 8), ("g_ca", 8), ("g_mem", 8), ("g_ffn2", 8),
                   ("conv_b", 4), ("ln_g", 4), ("ln_b", 4), ("gn_g", 4), ("conv_w", 4 * CONV_K)):
        PCOL[(_n, _l)] = _c
        _c += _w
PCOL[("g_final", 0)] = _c
_c += 8
NPCOL = _c

TC_MASKP = 0
TC_QDEC = 512
TC_KDEC = 1024
TC_MASKS = 1028
TC_QDECS = 1284
TC_KDECS = 1540
TC_ONEH = 1544
NTCOL = 1560


def _fm(v):
    return np.ascontiguousarray(np.asarray(v, np.float32).reshape(-1, 128).T)


def pack_params(inp):
    P = np.zeros((128, NPCOL), np.float32)
    for l in range(2):
        for n, src in (("g_ffn1", "g_ffn1"), ("g_mix", "g_mix"), ("g_ca", "g_ca"), ("g_mem", "g_mem"),
                       ("g_ffn2", "g_ffn2"), ("conv_b", "conv_b"), ("ln_g", "conv_ln_g"), ("ln_b", "conv_ln_b")):
            a = _fm(inp[src][l])
            P[:, PCOL[(n, l)]:PCOL[(n, l)] + a.shape[1]] = a
        P[:, PCOL[("gn_g", l)]:PCOL[("gn_g", l)] + 4] = np.asarray(inp["ret_gn_g"][l], np.float32).T
        cw = np.asarray(inp["conv_w"][l], np.float32)
        cw = cw.reshape(CONV_K, 4, 128).transpose(2, 1, 0)
        P[:, PCOL[("conv_w", l)]:PCOL[("conv_w", l)] + 4 * CONV_K] = cw.reshape(128, 4 * CONV_K)
    P[:, PCOL[("g_final", 0)]:PCOL[("g_final", 0)] + 8] = _fm(inp["g_final"])
    return P


def const_tables():
    f32 = np.float32
    inv_freq = 10000.0 ** (-np.arange(0, 128, 2, dtype=np.float64) / 128.0)
    pos = np.concatenate([np.arange(SEQ, dtype=np.float64),
                          np.tile(float(PAST_LEN) + np.arange(4, dtype=np.float64), NB_S)])
    ang = pos[:, None] * inv_freq[None, :]
    cos = np.cos(ang).astype(f32).T
    sin = np.sin(ang).astype(f32).T
    cosT = np.concatenate([cos, cos], 0)
    sinT = np.concatenate([-sin, sin], 0)
    rope = np.ascontiguousarray(np.stack([cosT, sinT], 1)).astype(f32)

    T = np.zeros((128, NTCOL), np.float64)
    lg = np.log1p(-np.exp2(-5.0 - np.arange(4, dtype=np.float64)))
    sc = 128.0 ** -0.5
    j = np.arange(128)[:, None]
    i = np.arange(128)[None, :]
    for h in range(4):
        m = np.where(i >= j, np.exp(-lg[h] * (j + 1.0)), 0.0) * sc
        T[:, TC_MASKP + h * 128:TC_MASKP + (h + 1) * 128] = m
        T[:, TC_QDEC + h * 128:TC_QDEC + (h + 1) * 128] = np.exp(lg[h] * (i + 1.0))
        T[:, TC_KDEC + h] = np.exp(lg[h] * (127.0 - np.arange(128))) * sc
    js = np.arange(64)[:, None]
    is_ = np.arange(64)[None, :]
    lj, li = js % 4, is_ % 4
    same = (js // 4) == (is_ // 4)
    for h in range(4):
        m = np.where(same & (li >= lj), np.exp(-lg[h] * (lj + 1.0)), 0.0) * sc
        T[:64, TC_MASKS + h * 64:TC_MASKS + (h + 1) * 64] = m
        T[:, TC_QDECS + h * 64:TC_QDECS + (h + 1) * 64] = np.exp(lg[h] * (li + 1.0))
        T[:64, TC_KDECS + h] = np.exp(lg[h] * (3.0 - (np.arange(64) % 4))) * sc
    T[:64, TC_ONEH:TC_ONEH + 16] = (np.arange(64)[:, None] // 4 == np.arange(16)[None, :])
    mats = np.zeros((128, 256), f32)
    mats[:, 0:128] = np.eye(128, dtype=f32)
    k = np.arange(128)[:, None]
    d = np.arange(128)[None, :]
    mats[:, 128:256] = (k == (d + 64) % 128)
    gam = np.exp(lg)
    return rope, T.astype(f32), mats, gam


class Kern:
    def __init__(self, nc, s, st, wplan, stop_after):
        self.nc, self.s, self.st = nc, s, st
        self.wplan_in = wplan
        self.wplan_out = []
        self.wissued = 0
        self.wreleased = 0
        self.wcount = 0
        self.stop_after = stop_after
        self.gam = const_tables()[3]
        self._declare()

    def _declare(self):
        nc, st = self.nc, self.st

        def din(name, shape):
            return nc.dram_tensor(name, list(shape), F32, kind="ExternalInput").ap()

        def dout(name, shape):
            return nc.dram_tensor(name, list(shape), F32, kind="ExternalOutput").ap()

        self.d = d = {}
        d["xp"] = din("xp", (SEQ, D)); d["xs"] = din("xs", (NS_TOK, D))
        d["sconv"] = din("sconv", (2, NB_S, 30, 512)); d["sret"] = din("sret", (2, NB_S, 4, 128, 128))
        d["cmk"] = din("cmk", (2, NB_S, NMEM, D)); d["cmv"] = din("cmv", (2, NB_S, NMEM, D))
        d["memp"] = din("memp", (NMEM, D))
        for n, shp in (("w1a", (2, D, DFF)), ("w3a", (2, D, DFF)), ("w2a", (2, DFF, D)), ("w_in", (2, D, 3072)),
                       ("w_out", (2, D, D)), ("w_cq", (2, D, D)), ("w_ck", (2, D, D)), ("w_cv", (2, D, D)),
                       ("w_co", (2, D, D)), ("w1b", (2, D, DFF)), ("w3b", (2, D, DFF)), ("w2b", (2, DFF, D))):
            d[n] = din(n, shp)
        d["params"] = din("params", (128, NPCOL)); d["rope"] = din("rope", (128, 2, NT))
        d["tabs"] = din("tabs", (128, NTCOL)); d["mats"] = din("mats", (128, 256))
        d["yp"] = dout("yp", (SEQ, D)); d["ys"] = dout("ys", (NS_TOK, D))
        d["ncp"] = dout("ncp", (2, 30, 512)); d["nrp"] = dout("nrp", (2, 4, 128, 128))
        d["nmk"] = dout("nmk", (2, NMEM, D)); d["nmv"] = dout("nmv", (2, NMEM, D))
        d["ncs"] = dout("ncs", (2, NB_S, 30, 512)); d["nrs"] = dout("nrs", (2, NB_S, 4, 128, 128))

        def sb(name, shape, dt):
            return st.enter_context(nc.sbuf_tensor(name, list(shape), dt))

        self.xT = sb("xT", (128, 8, NT), F32)
        self.hT = sb("hT", (128, 8, NT), BF16)
        self.params = sb("params_sb", (128, NPCOL), F32)
        self.tabs = sb("tabs_sb", (128, NTCOL), F32)
        self.identf = sb("identf", (128, 128), F32)
        self.identb = sb("identb", (128, 128), BF16)
        self.rotP = sb("rotP", (128, 128), BF16)
        self.ones_rms = sb("ones_rms", (128, 128), BF16)
        self.ones_b = sb("ones_b", (128, 128), BF16)
        self.ones_ln = sb("ones_ln", (128, 128), F32)
        self.ones_gn = sb("ones_gn", (128, 128), F32)
        self.ones_lnb = sb("ones_lnb", (128, 128), BF16)
        self.ones_gnb = sb("ones_gnb", (128, 128), BF16)
        self.ring = [sb("ring%d" % i, (128, 4096), BF16) for i in range(NRING)]
        self.ARENA_BYTES = 74 * 1024
        self.arena = sb("arena", (128, self.ARENA_BYTES // 4), F32)
        self.psum = [st.enter_context(nc.psum_tensor("ps%d" % i, [128, 512], F32)) for i in range(8)]
        self.Lps = [LT("ps%d" % i) for i in range(8)]
        self.ps_i = 0
        self.LxT = [[LT("xT%d_%d" % (c, t)) for t in range(5)] for c in range(8)]
        self.LhT = [[LT("hT%d_%d" % (c, t)) for t in range(5)] for c in range(8)]
        self.Lring = [LT("ring%d" % i) for i in range(NRING)]
        self.Dring = [self.s.dsem("ring%d" % i) for i in range(NRING)]
        self.Lconst = LT("const")
        self.live = []
        self.retired = []
        self.aoff = 0

    def phase_reset(self, keep=0):
        self.retired.extend([x for x in self.live if x[0] >= keep])
        self.live = [x for x in self.live if x[0] < keep]
        self.aoff = keep

    def alloc(self, nbytes, nlt=1, name="a"):
        nbytes = (nbytes + 63) // 64 * 64
        off = self.aoff
        assert off + nbytes <= self.ARENA_BYTES, ("arena overflow", name, off, nbytes)
        self.aoff += nbytes
        lts = [LT("%s%d" % (name, i)) for i in range(nlt)]
        olds = []
        for (o, sz, ls) in self.retired:
            if o < off + nbytes and off < o + sz:
                olds.extend(ls)
        for n_ in lts:
            self.s.lt_inherit(n_, olds)
        self.live.append((off, nbytes, lts))
        return off, lts

    def vf32(self, off, n):
        return self.arena[:, off // 4: off // 4 + n]

    def vbf(self, off, n):
        return self.arena[:, off // 4: off // 4 + (n + 1) // 2].bitcast(BF16)[:, 0:n]

    def next_ps(self):
        i = self.ps_i
        self.ps_i = (i + 1) % 8
        return self.psum[i], self.Lps[i]

    def pc(self, name, l, c, n=1):
        o = PCOL[(name, l)] + c
        return self.params[:, o:o + n]

    def _issue_w(self, i, desc):
        slot = i % NRING
        ring, L, dsem = self.ring[slot], self.Lring[slot], self.Dring[slot]
        kind = desc[0]
        d = self.d
        if kind == "A":
            _, name, l, c0 = desc
            src = d[name][l][:, c0:c0 + 512].rearrange("(k p) n -> p k n", p=128)
            dst = ring[:].rearrange("p (k n) -> p k n", k=8)
            self.s.dma("pool", lambda e, h: e.dma_start(out=dst, in_=src).then_inc(h, 16), dsem, 1, w=[L], c=8.0)
        elif kind == "B":
            _, name, l, r0 = desc
            src = d[name][l][r0:r0 + 512, :].rearrange("(f p) n -> p f n", p=128)
            dst = ring[:].rearrange("p (f n) -> p f n", f=4)
            self.s.dma("pool", lambda e, h: e.dma_start(out=dst, in_=src).then_inc(h, 16), dsem, 1, w=[L], c=8.0)
        elif kind == "H":
            _, l, hh = desc
            dst = ring[:].rearrange("p (t k n) -> p t k n", t=4, k=8)

            def fn(e, h):
                for ty in range(4):
                    c0 = 1024 + ty * 512 + hh * 128
                    src = d["w_in"][l][:, c0:c0 + 128].rearrange("(k p) n -> p k n", p=128)
                    e.dma_start(out=dst[:, ty], in_=src).then_inc(h, 16)
            self.s.dma("pool", fn, dsem, 4, w=[L], c=8.0)

    def wblock(self, desc):
        i = self.wcount
        self.wcount += 1
        self.wplan_out.append(desc)
        assert i - NRING < self.wreleased, ("weight ring too small", i, self.wreleased)
        if self.wplan_in is not None:
            assert self.wplan_in[i] == desc
        while self.wissued <= i:
            self._issue_w(self.wissued, desc if self.wplan_in is None else self.wplan_in[self.wissued])
            self.wissued += 1
        slot = i % NRING
        return self.ring[slot], self.Lring[slot]

    def wrelease(self, n=1):
        self.wreleased += n
        assert self.wreleased <= self.wcount
        if self.wplan_in is not None:
            hi = min(len(self.wplan_in), self.wreleased + NRING)
            while self.wissued < hi:
                self._issue_w(self.wissued, self.wplan_in[self.wissued])
                self.wissued += 1

    def load_consts(self):
        s, d = self.s, self.d
        dsem = s.dsem("const")
        off_m, Lm = self.alloc(256 * 4, 1, "mats")
        mats = self.vf32(off_m, 256)

        def fn(e, h):
            e.dma_start(out=self.params[:], in_=d["params"]).then_inc(h, 16)
            e.dma_start(out=self.tabs[:], in_=d["tabs"]).then_inc(h, 16)
            e.dma_start(out=mats, in_=d["mats"]).then_inc(h, 16)
        s.dma("sp", fn, dsem, 3, w=[self.Lconst, Lm[0]])

        def cp(e):
            e.tensor_copy(out=self.identf[:], in_=mats[:, 0:128])
            e.tensor_copy(out=self.identb[:], in_=mats[:, 0:128])
            e.tensor_copy(out=self.rotP[:], in_=mats[:, 128:256])
            e.memset(self.ones_rms[:], 1.0 / 1024)
            e.memset(self.ones_b[:], 1.0)
            e.memset(self.ones_ln[:], 1.0 / 512)
            e.memset(self.ones_lnb[:], 1.0 / 512)
            e.memset(self.ones_gnb[:], 1.0 / 128)
            return e.memset(self.ones_gn[:], 1.0 / 128)
        s.op("dve", cp, r=[Lm[0]], w=[self.Lconst])

    def load_x(self):
        s, d = self.s, self.d
        self.phase_reset()
        offs, Ls = [], []
        for i in range(4):
            o, L = self.alloc(4096, 1, "xstage")
            offs.append(o); Ls.append(L[0])
        ds = [s.dsem("xst%d" % i) for i in range(4)]
        for tt in range(17):
            n = 128 if tt < 16 else 64
            sl = tt % 4
            stg = self.vf32(offs[sl], 1024)
            src = d["xp"][tt * 128:(tt + 1) * 128, :] if tt < 16 else d["xs"]
            s.dma("sp", lambda e, h, stg=stg, src=src, n=n: e.dma_start(out=stg[0:n, :], in_=src).then_inc(h, 16),
                  ds[sl], 1, w=[Ls[sl]])
            t = tt // 4 if tt < 16 else 4
            for half in range(2):
                ps, Lp = self.next_ps()
                for j in range(4):
                    c = half * 4 + j
                    s.op("pe", lambda e, ps=ps, j=j, c=c, stg=stg, n=n: e.transpose(
                        ps[:, j * 128:j * 128 + n], stg[0:n, c * 128:(c + 1) * 128], self.identf[0:n, 0:n]),
                        r=[Ls[sl], self.Lconst], w=[Lp], c=PEC(n))
                tok0 = tt * 128
                dst = self.xT[:, half * 4:half * 4 + 4, tok0:tok0 + n]
                srcp = ps[:].rearrange("p (c t) -> p c t", c=4)[:, :, 0:n]
                eng = "act" if half == 0 else "dve"
                if eng == "act":
                    s.op("act", lambda e, dst=dst, srcp=srcp: e.activation(out=dst, in_=srcp, func=AF.Copy),
                         r=[Lp], w=[self.LxT[c][t] for c in range(half * 4, half * 4 + 4)])
                else:
                    s.op("dve", lambda e, dst=dst, srcp=srcp: e.tensor_copy(out=dst, in_=srcp),
                         r=[Lp], w=[self.LxT[c][t] for c in range(half * 4, half * 4 + 4)])

    def rmsnorm(self, gname, l):
        s = self.s
        self.phase_reset()
        sq_o, sq_L = [], []
        for i in range(3):
            o, L = self.alloc(1024, 1, "sq")
            sq_o.append(o); sq_L.append(L[0])
        rs_o, rs_L = [], []
        for i in range(2):
            o, L = self.alloc(2048, 1, "rstd")
            rs_o.append(o); rs_L.append(L[0])
        qi = 0
        for t, (t0, n) in enumerate(TT):
            ps, Lp = self.next_ps()
            for c in range(8):
                sq = self.vbf(sq_o[qi % 3], 512)[:, 0:n]
                Lq = sq_L[qi % 3]
                qi += 1
                s.op("act", lambda e, sq=sq, c=c, t0=t0, n=n: e.activation(out=sq, in_=self.xT[:, c, t0:t0 + n], func=AF.Square),
                     r=[self.LxT[c][t]], w=[Lq])
                s.op("pe", lambda e, ps=ps, sq=sq, c=c, n=n: e.matmul(ps[:, 0:n], lhsT=self.ones_rms[:], rhs=sq, start=(c == 0), stop=(c == 7)),
                     r=[Lq, self.Lconst], w=[Lp], c=PEC(n))
            rstd = self.vf32(rs_o[t % 2], 512)[:, 0:n]
            Lr = rs_L[t % 2]
            s.op("act", lambda e, rstd=rstd, ps=ps, n=n: e.activation(out=rstd, in_=ps[:, 0:n], func=AF.Ln, bias=EPS, scale=1.0),
                 r=[Lp], w=[Lr])
            s.op("act", lambda e, rstd=rstd: e.activation(out=rstd, in_=rstd, func=AF.Exp, scale=-0.5), r=[Lr], w=[Lr])
            for c in range(8):
                s.op("dve", lambda e, c=c, t0=t0, n=n, rstd=rstd: e.scalar_tensor_tensor(
                    out=self.hT[:, c, t0:t0 + n], in0=self.xT[:, c, t0:t0 + n], scalar=self.pc(gname, l, c), in1=rstd,
                    op0=ALU.mult, op1=ALU.mult), r=[self.LxT[c][t], Lr, self.Lconst], w=[self.LhT[c][t]])

    def proj_acc(self, wdescs, in_fn, nk, scale):
        s = self.s
        for half in range(2):
            ring, Lw = self.wblock(wdescs[half])
            wv = ring[:].rearrange("p (k n) -> p k n", k=8)
            for t, (t0, n) in enumerate(TT):
                for dcl in range(4):
                    dc = half * 4 + dcl
                    ps, Lp = self.next_ps()
                    for k in range(nk):
                        rhs, Lr = in_fn(k, t)
                        s.op("pe", lambda e, ps=ps, wv=wv, k=k, dcl=dcl, rhs=rhs, n=n: e.matmul(
                            ps[:, 0:n], lhsT=wv[:, k, dcl * 128:(dcl + 1) * 128], rhs=rhs, start=(k == 0), stop=(k == nk - 1)),
                            r=[Lw, Lr], w=[Lp], c=PEC(n))
                    s.op("dve", lambda e, ps=ps, dc=dc, t0=t0, n=n: e.scalar_tensor_tensor(
                        out=self.xT[:, dc, t0:t0 + n], in0=ps[:, 0:n], scalar=float(scale), in1=self.xT[:, dc, t0:t0 + n],
                        op0=ALU.mult, op1=ALU.add), r=[Lp, self.LxT[dc][t]], w=[self.LxT[dc][t]])
            self.wrelease(1)

    def ffn(self, l, which):
        s = self.s
        self.phase_reset()
        w1, w3, w2 = "w1" + which, "w3" + which, "w2" + which
        g_o, g_L = self.alloc(4 * NT * 2, 20, "gT")
        gT = self.vbf(g_o, 4 * NT).rearrange("p (f t) -> p f t", f=4)
        sg_o, sg_L = [], []
        for i in range(2):
            o, L = self.alloc(2048, 1, "sg")
            sg_o.append(o); sg_L.append(L[0])
        si = 0
        for fg in range(8):
            r1, L1 = self.wblock(("A", w1, l, fg * 512))
            r3, L3 = self.wblock(("A", w3, l, fg * 512))
            r2, L2 = self.wblock(("B", w2, l, fg * 512))
            v1 = r1[:].rearrange("p (k n) -> p k n", k=8)
            v3 = r3[:].rearrange("p (k n) -> p k n", k=8)
            v2 = r2[:].rearrange("p (f n) -> p f n", f=4)
            for fc in range(4):
                for t, (t0, n) in enumerate(TT):
                    pa, La = self.next_ps()
                    pb, Lb = self.next_ps()
                    for (pp, Lpp, vv, Lw) in ((pa, La, v1, L1), (pb, Lb, v3, L3)):
                        for k in range(8):
                            s.op("pe", lambda e, pp=pp, vv=vv, k=k, fc=fc, t0=t0, n=n: e.matmul(
                                pp[:, 0:n], lhsT=vv[:, k, fc * 128:(fc + 1) * 128], rhs=self.hT[:, k, t0:t0 + n],
                                start=(k == 0), stop=(k == 7)), r=[Lw, self.LhT[k][t]], w=[Lpp], c=PEC(n))
                    sg = self.vf32(sg_o[si % 2], 512)[:, 0:n]
                    Ls = sg_L[si % 2]
                    si += 1
                    s.op("act", lambda e, sg=sg, pa=pa, n=n: e.activation(out=sg, in_=pa[:, 0:n], func=AF.Silu), r=[La], w=[Ls])
                    s.op("dve", lambda e, sg=sg, pb=pb, fc=fc, t0=t0, n=n: e.tensor_tensor(
                        out=gT[:, fc, t0:t0 + n], in0=sg, in1=pb[:, 0:n], op=ALU.mult), r=[Ls, Lb], w=[g_L[fc * 5 + t]])
            self.wrelease(2)
            for t, (t0, n) in enumerate(TT):
                for dc in range(8):
                    ps, Lp = self.next_ps()
                    for fc in range(4):
                        s.op("pe", lambda e, ps=ps, fc=fc, dc=dc, t0=t0, n=n, v2=v2: e.matmul(
                            ps[:, 0:n], lhsT=v2[:, fc, dc * 128:(dc + 1) * 128], rhs=gT[:, fc, t0:t0 + n],
                            start=(fc == 0), stop=(fc == 3)), r=[L2, g_L[fc * 5 + t]], w=[Lp], c=PEC(n))
                    s.op("dve", lambda e, ps=ps, dc=dc, t0=t0, n=n: e.scalar_tensor_tensor(
                        out=self.xT[:, dc, t0:t0 + n], in0=ps[:, 0:n], scalar=0.5, in1=self.xT[:, dc, t0:t0 + n],
                        op0=ALU.mult, op1=ALU.add), r=[Lp, self.LxT[dc][t]], w=[self.LxT[dc][t]])
            self.wrelease(1)

    def conv(self, l):
        s, d = self.s, self.d
        self.phase_reset()
        UP = SEQ + 30
        up_o, up_L = self.alloc(4 * UP * 2, 4, "up")
        us_o, us_L = self.alloc(4 * 16 * 34 * 4, 4, "us")
        u32_o, u32_L = self.alloc(4 * 30 * 4, 1, "u32p")
        up = self.vbf(up_o, 4 * UP).rearrange("p (c t) -> p c t", c=4)
        us = self.vf32(us_o, 4 * 16 * 34).rearrange("p (c b t) -> p c b t", c=4, b=16)
        u32p = self.vf32(u32_o, 120).rearrange("p (c t) -> p c t", c=4)
        sg_o, sg_L = [], []
        for i in range(2):
            o, L = self.alloc(2048, 1, "sgm")
            sg_o.append(o); sg_L.append(L[0])
        st2_o, st2_L = self.alloc(2048, 1, "cstage2")
        st2 = self.vf32(st2_o, 512)
        y_o, y_L = self.alloc(4 * NT * 4, 20, "y")
        y = self.vf32(y_o, 4 * NT).rearrange("p (c t) -> p c t", c=4)
        keep = self.aoff
        stg_o, stg_L = self.alloc(4 * 512 * 4, 1, "cstage")
        stg = self.vf32(stg_o, 2048).rearrange("p (g c) -> p g c", g=4)
        dst_in = s.dsem("cst_in%d" % l)
        dst_out = s.dsem("cst_out%d" % l)
        dst_out2 = s.dsem("cst_outp%d" % l)
        src = d["sconv"][l].rearrange("(g bb) t c -> (bb t) g c", bb=4)
        s.dma("sp", lambda e, h: e.dma_start(out=stg[0:120], in_=src).then_inc(h, 16), dst_in, 1, w=[stg_L[0]])
        for cc in range(4):
            ps, Lp = self.next_ps()
            for g in range(4):
                s.op("pe", lambda e, ps=ps, g=g, cc=cc: e.transpose(ps[:, g * 120:(g + 1) * 120], stg[0:120, g, cc * 128:(cc + 1) * 128],
                                                                   self.identf[0:120, 0:120]), r=[stg_L[0], self.Lconst], w=[Lp])
            s.op("act", lambda e, ps=ps, cc=cc: e.activation(out=us[:, cc, :, 0:30], in_=ps[:, 0:480].rearrange("p (b t) -> p b t", b=16),
                                                             func=AF.Copy), r=[Lp], w=[us_L[cc]])
        s.op("dve", lambda e: e.memset(up[:, :, 0:30], 0.0), w=up_L)
        ra, La_ = self.wblock(("A", "w_in", l, 0))
        rg, Lg_ = self.wblock(("A", "w_in", l, 512))
        va = ra[:].rearrange("p (k n) -> p k n", k=8)
        vg = rg[:].rearrange("p (k n) -> p k n", k=8)
        si = 0
        for cc in range(4):
            for t, (t0, n) in enumerate(TT):
                pa, La = self.next_ps()
                pg, Lg = self.next_ps()
                for (pp, Lpp, vv, Lw) in ((pa, La, va, La_), (pg, Lg, vg, Lg_)):
                    for k in range(8):
                        s.op("pe", lambda e, pp=pp, vv=vv, k=k, cc=cc, t0=t0, n=n: e.matmul(
                            pp[:, 0:n], lhsT=vv[:, k, cc * 128:(cc + 1) * 128], rhs=self.hT[:, k, t0:t0 + n],
                            start=(k == 0), stop=(k == 7)), r=[Lw, self.LhT[k][t]], w=[Lpp], c=PEC(n))
                sg = self.vf32(sg_o[si % 2], 512)[:, 0:n]
                Ls = sg_L[si % 2]
                si += 1
                s.op("act", lambda e, sg=sg, pg=pg, n=n: e.activation(out=sg, in_=pg[:, 0:n], func=AF.Sigmoid), r=[Lg], w=[Ls])
                if t < 4:
                    s.op("dve", lambda e, sg=sg, pa=pa, cc=cc, t0=t0: e.tensor_tensor(
                        out=up[:, cc, 30 + t0:30 + t0 + 512], in0=pa[:, 0:512], in1=sg, op=ALU.mult), r=[La, Ls], w=[up_L[cc]])
                    if t == 3:
                        s.op("dve", lambda e, sg=sg, pa=pa, cc=cc: e.tensor_tensor(
                            out=u32p[:, cc, :], in0=pa[:, 482:512], in1=sg[:, 482:512], op=ALU.mult), r=[La, Ls], w=[u32_L[0]])
                else:
                    s.op("dve", lambda e, sg=sg, pa=pa, cc=cc: e.tensor_tensor(
                        out=us[:, cc, :, 30:34], in0=pa[:, 0:64].rearrange("p (b t) -> p b t", b=16),
                        in1=sg.rearrange("p (b t) -> p b t", b=16), op=ALU.mult), r=[La, Ls], w=[us_L[cc]])
        self.wrelease(2)
        psg = [self.next_ps() for g in range(4)]
        for cc in range(4):
            cont = self.vf32(sg_o[cc % 2], 512)[:, 0:480]
            Lc = sg_L[cc % 2]
            s.op("dve", lambda e, cont=cont, cc=cc: e.tensor_copy(out=cont.rearrange("p (b t) -> p b t", b=16), in_=us[:, cc, :, 4:34]),
                 r=[us_L[cc]], w=[Lc])
            for g in range(4):
                ps, Lp = psg[g]
                s.op("pe", lambda e, ps=ps, g=g, cc=cc, cont=cont: e.transpose(ps[0:120, cc * 128:(cc + 1) * 128], cont[:, g * 120:(g + 1) * 120],
                                                                              self.identf[:]), r=[Lc, self.Lconst], w=[Lp])
        for g in range(4):
            ps, Lp = psg[g]
            s.op("act", lambda e, ps=ps, g=g: e.activation(out=stg[0:120, g, :], in_=ps[0:120, :], func=AF.Copy), r=[Lp], w=[stg_L[0]])
        dsto = d["ncs"][l].rearrange("(g bb) t c -> (bb t) g c", bb=4)
        s.dma("sp", lambda e, h: e.dma_start(out=dsto, in_=stg[0:120]).then_inc(h, 16), dst_out, 1, r=[stg_L[0]], w=[LT()])
        ps, Lp = self.next_ps()
        for cc in range(4):
            s.op("pe", lambda e, ps=ps, cc=cc: e.transpose(ps[0:30, cc * 128:(cc + 1) * 128], u32p[:, cc, :], self.identf[:]),
                 r=[u32_L[0], self.Lconst], w=[Lp])
        s.op("act", lambda e, ps=ps: e.activation(out=st2[0:30, :], in_=ps[0:30, :], func=AF.Copy), r=[Lp], w=[st2_L[0]])
        s.dma("sp", lambda e, h: e.dma_start(out=d["ncp"][l], in_=st2[0:30, :]).then_inc(h, 16), dst_out2, 1, r=[st2_L[0]], w=[LT()])
        cw0 = PCOL[("conv_w", l)]
        self.phase_reset(keep=keep)
        dg_o, dg_L = self.alloc(CONV_K * 128 * 2, 1, "diag")
        diag = self.vbf(dg_o, CONV_K * 128).rearrange("p (j d) -> p j d", j=CONV_K)
        for cc in range(4):
            s.op("dve", lambda e, cc=cc: e.tensor_tensor(
                out=diag, in0=self.identb[:].unsqueeze(1).to_broadcast([128, CONV_K, 128]),
                in1=self.params[:, cw0 + cc * CONV_K:cw0 + (cc + 1) * CONV_K].unsqueeze(2).to_broadcast([128, CONV_K, 128]), op=ALU.mult),
                r=[self.Lconst], w=[dg_L[0]])
            for t in range(4):
                t0 = TT[t][0]
                ps, Lp = self.next_ps()
                for j in range(CONV_K):
                    s.op("pe", lambda e, ps=ps, j=j, cc=cc, t0=t0: e.matmul(ps[:], lhsT=diag[:, j, :], rhs=up[:, cc, t0 + j:t0 + j + 512],
                                                                         start=(j == 0), stop=(j == CONV_K - 1)), r=[dg_L[0], up_L[cc]], w=[Lp])
                s.op("act", lambda e, ps=ps, cc=cc, t0=t0: e.activation(out=y[:, cc, t0:t0 + 512], in_=ps[:], func=AF.Identity,
                                                                      bias=self.pc("conv_b", l, cc), scale=1.0), r=[Lp, self.Lconst], w=[y_L[cc * 5 + t]])
            yv = y[:, cc, SEQ:SEQ + 64].rearrange("p (b t) -> p b t", b=16)
            Ly = y_L[cc * 5 + 4]
            s.op("dve", lambda e, yv=yv, cc=cc: e.tensor_scalar(
                out=yv, in0=us[:, cc, :, 0:4], scalar1=self.params[:, cw0 + cc * CONV_K:cw0 + cc * CONV_K + 1],
                scalar2=self.pc("conv_b", l, cc), op0=ALU.mult, op1=ALU.add), r=[us_L[cc], self.Lconst], w=[Ly])
            for j in range(1, CONV_K):
                s.op("dve", lambda e, yv=yv, cc=cc, j=j: e.scalar_tensor_tensor(
                    out=yv, in0=us[:, cc, :, j:j + 4], scalar=self.params[:, cw0 + cc * CONV_K + j:cw0 + cc * CONV_K + j + 1], in1=yv,
                    op0=ALU.mult, op1=ALU.add), r=[us_L[cc], self.Lconst], w=[Ly])
        self.retired.append((up_o, 4 * UP * 2, up_L))
        self.retired.append((us_o, 4 * 16 * 34 * 4, us_L))
        co_L = [LT("co%d" % i) for i in range(20)]
        for n_ in co_L:
            self.s.lt_inherit(n_, up_L + us_L)
        co = self.vbf(up_o, 4 * NT).rearrange("p (c t) -> p c t", c=4)
        assert 4 * NT * 2 <= 4 * UP * 2 + 4 * 16 * 34 * 4
        self.phase_reset(keep=keep)
        tm_o, tm_L = [], []
        for i in range(4):
            o, L = self.alloc(2048, 1, "lnt")
            tm_o.append(o); tm_L.append(L[0])
        for t, (t0, n) in enumerate(TT):
            pm, Lm = self.next_ps()
            pe2, Le = self.next_ps()
            for cc in range(4):
                sq = self.vbf(sg_o[cc % 2], 512)[:, 0:n]
                yb = self.vbf(sg_o[cc % 2] + 1024, 512)[:, 0:n]
                Lq = sg_L[cc % 2]
                s.op("act", lambda e, sq=sq, cc=cc, t0=t0, n=n: e.activation(out=sq, in_=y[:, cc, t0:t0 + n], func=AF.Square),
                     r=[y_L[cc * 5 + t]], w=[Lq])
                s.op("act", lambda e, yb=yb, cc=cc, t0=t0, n=n: e.activation(out=yb, in_=y[:, cc, t0:t0 + n], func=AF.Copy),
                     r=[y_L[cc * 5 + t], Lq], w=[Lq])
                s.op("pe", lambda e, pm=pm, yb=yb, cc=cc, n=n: e.matmul(pm[:, 0:n], lhsT=self.ones_lnb[:], rhs=yb,
                                                                       start=(cc == 0), stop=(cc == 3)), r=[Lq, self.Lconst], w=[Lm], c=PEC(n))
                s.op("pe", lambda e, pe2=pe2, sq=sq, cc=cc, n=n: e.matmul(pe2[:, 0:n], lhsT=self.ones_lnb[:], rhs=sq,
                                                                          start=(cc == 0), stop=(cc == 3)), r=[Lq, self.Lconst], w=[Le], c=PEC(n))
            mean, rstd, z = [self.vf32(tm_o[i], 512)[:, 0:n] for i in range(3)]
            Lmean, Lrstd, Lz = tm_L[0], tm_L[1], tm_L[2]
            self.norm_stats(pm, Lm, pe2, Le, mean, Lmean, rstd, Lrstd, n, EPS)
            for cc in range(4):
                s.op("dve", lambda e, z=z, mean=mean, cc=cc, t0=t0, n=n: e.tensor_tensor(out=z, in0=y[:, cc, t0:t0 + n], in1=mean, op=ALU.subtract),
                     r=[y_L[cc * 5 + t], Lmean], w=[Lz])
                s.op("dve", lambda e, z=z, rstd=rstd: e.tensor_tensor(out=z, in0=z, in1=rstd, op=ALU.mult), r=[Lrstd], w=[Lz])
                s.op("act", lambda e, z=z, cc=cc, t0=t0, n=n: e.activation(out=co[:, cc, t0:t0 + n], in_=z, func=AF.Silu,
                                                                           bias=self.pc("ln_b", l, cc), scale=self.pc("ln_g", l, cc)),
                     r=[Lz, self.Lconst], w=[co_L[cc * 5 + t]])
        self.proj_acc_rows("w_out", l, 0, lambda k, t: (co[:, k, TT[t][0]:TT[t][0] + TT[t][1]], co_L[k * 5 + t]), 1.0)
        self.retired.append((up_o, 4 * NT * 2, co_L))

    def norm_stats(self, pm, Lm, pe2, Le, mean, Lmean, rstd, Lrstd, n, eps):
        s = self.s
        s.op("act", lambda e: e.activation(out=mean, in_=pm[:, 0:n], func=AF.Copy), r=[Lm], w=[Lmean])
        s.op("dve", lambda e: e.tensor_tensor(out=rstd, in0=mean, in1=mean, op=ALU.mult), r=[Lmean], w=[Lrstd])
        s.op("dve", lambda e: e.tensor_tensor(out=rstd, in0=pe2[:, 0:n], in1=rstd, op=ALU.subtract), r=[Le], w=[Lrstd])
        s.op("act", lambda e: e.activation(out=rstd, in_=rstd, func=AF.Ln, bias=float(eps), scale=1.0), r=[Lrstd], w=[Lrstd])
        s.op("act", lambda e: e.activation(out=rstd, in_=rstd, func=AF.Exp, scale=-0.5), r=[Lrstd], w=[Lrstd])

    def proj_acc_rows(self, wname, l, r0, in_fn, scale):
        s = self.s
        ring, Lw = self.wblock(("B", wname, l, r0))
        wv = ring[:].rearrange("p (f n) -> p f n", f=4)
        for t, (t0, n) in enumerate(TT):
            for dc in range(8):
                ps, Lp = self.next_ps()
                for k in range(4):
                    rhs, Lr = in_fn(k, t)
                    s.op("pe", lambda e, ps=ps, k=k, dc=dc, rhs=rhs, n=n: e.matmul(
                        ps[:, 0:n], lhsT=wv[:, k, dc * 128:(dc + 1) * 128], rhs=rhs, start=(k == 0), stop=(k == 3)),
                        r=[Lw, Lr], w=[Lp], c=PEC(n))
                s.op("dve", lambda e, ps=ps, dc=dc, t0=t0, n=n: e.scalar_tensor_tensor(
                    out=self.xT[:, dc, t0:t0 + n], in0=ps[:, 0:n], scalar=float(scale), in1=self.xT[:, dc, t0:t0 + n],
                    op0=ALU.mult, op1=ALU.add), r=[Lp, self.LxT[dc][t]], w=[self.LxT[dc][t]])
        self.wrelease(1)

    def retention(self, l):
        s, d = self.s, self.d
        self.phase_reset()
        ro_o, ro_L = self.alloc(4 * NT * 2, 20, "retout")
        ro = self.vbf(ro_o, 4 * NT).rearrange("p (c t) -> p c t", c=4)
        rp_o, rp_L = [], []
        for i in range(2):
            o, L = self.alloc(2 * 512 * 4, 1, "rope")
            rp_o.append(o); rp_L.append(L[0])
        drope = [s.dsem("rope%d_%d" % (l, i)) for i in range(2)]
        q_o, q_L = self.alloc(NT * 2, 5, "qT")
        k_o, k_L = self.alloc(NT * 2, 5, "kT")
        gt_o, gt_L = self.alloc(NT * 2, 5, "gate")
        v_o, v_L = self.alloc(17 * 128 * 2, 5, "vtok")
        kd_o, kd_L = self.alloc(17 * 128 * 2, 5, "kdtok")
        s32_o, s32_L = self.alloc(128 * 4, 1, "S32")
        s0_o, s0_L = self.alloc(16 * 128 * 4, 1, "S0")
        q32_o, q32_L = self.alloc(64 * 4, 1, "q32s")
        qT = self.vbf(q_o, NT); kT = self.vbf(k_o, NT); gate = self.vbf(gt_o, NT)
        vtok = self.vbf(v_o, 17 * 128).rearrange("p (n d) -> p n d", n=17)
        kdtok = self.vbf(kd_o, 17 * 128).rearrange("p (n d) -> p n d", n=17)
        S32 = self.vf32(s32_o, 128)
        S0 = self.vf32(s0_o, 16 * 128).rearrange("p (b d) -> p b d", b=16)
        q32s = self.vf32(q32_o, 64)
        tmp_o, tmp_L = [], []
        for i in range(5):
            o, L = self.alloc(2048, 1, "rtmp")
            tmp_o.append(o); tmp_L.append(L[0])
        bt_o, bt_L = [], []
        for i in range(2):
            o, L = self.alloc(1024, 1, "rbt")
            bt_o.append(o); bt_L.append(L[0])
        st_o, st_L = [], []
        for i in range(2):
            o, L = self.alloc(1024, 1, "sTsb")
            st_o.append(o); st_L.append(L[0])
        keep = self.aoff
        ri = 0
        ds0 = s.dsem("s0in%d" % l)
        dso = s.dsem("soutp%d" % l)
        dso_s = s.dsem("souts%d" % l)
        tabs = self.tabs
        bi = 0
        for hh in range(4):
            if KDBG < 0:
                break
            ring, Lw = self.wblock(("H", l, hh))
            wv = ring[:].rearrange("p (t k n) -> p t k n", t=4, k=8)
            self.phase_reset(keep=keep)
            sb_o, sb_L = self.alloc(16 * 128 * 2, 16, "Sbf")
            Sbf = self.vbf(sb_o, 16 * 128).rearrange("p (n d) -> p n d", n=16)
            def ld_s0(e, h, hh=hh):
                for bg in range(4):
                    e.dma_start(out=S0[:, 4 * bg:4 * bg + 4, :], in_=d["sret"][l][4 * bg:4 * bg + 4, hh].rearrange("b k v -> k b v")).then_inc(h, 16)
            s.dma("sp", ld_s0, ds0, 4, w=[s0_L[0]])
            for t, (t0, n) in enumerate(TT):
                rope = self.vf32(rp_o[ri % 2], 1024).rearrange("p (a t) -> p a t", a=2)
                Lrope = rp_L[ri % 2]
                s.dma("sp", lambda e, h, rope=rope, t0=t0, n=n: e.dma_start(out=rope[:, :, 0:n], in_=d["rope"][:, :, t0:t0 + n]).then_inc(h, 16),
                      drope[ri % 2], 1, w=[Lrope])
                ri += 1
                if KDBG < 1:
                    continue
                for ty, dstT, dL in ((0, qT, q_L), (1, kT, k_L)):
                    ps, Lp = self.next_ps()
                    for k in range(8):
                        s.op("pe", lambda e, ps=ps, ty=ty, k=k, t0=t0, n=n, wv=wv: e.matmul(
                            ps[:, 0:n], lhsT=wv[:, ty, k, :], rhs=self.hT[:, k, t0:t0 + n], start=(k == 0), stop=(k == 7)),
                            r=[Lw, self.LhT[k][t]], w=[Lp], c=PEC(n))
                    if KSUB < 2:
                        continue
                    qb = self.vbf(bt_o[bi % 2], 512)[:, 0:n]
                    Lqb = bt_L[bi % 2]
                    bi += 1
                    s.op("act", lambda e, qb=qb, ps=ps, n=n: e.activation(out=qb, in_=ps[:, 0:n], func=AF.Copy), r=[Lp], w=[Lqb])
                    pr, Lpr = self.next_ps()
                    KV = int(os.environ.get("KVAR", "0"))
                    if KV == 0:
                        s.op("pe", lambda e, pr=pr, qb=qb, n=n: e.matmul(pr[:, 0:n], lhsT=self.rotP[:], rhs=qb, start=True, stop=True),
                             r=[Lqb, self.Lconst], w=[Lpr], c=PEC(n))
                    elif KV == 1:
                        s.op("pe", lambda e, pr=pr, qb=qb, n=n, wv=wv: e.matmul(pr[:, 0:n], lhsT=wv[:, 0, 0, :], rhs=qb, start=True, stop=True),
                             r=[Lqb, Lw], w=[Lpr], c=PEC(n))
                    elif KV == 2:
                        s.op("pe", lambda e, pr=pr, t0=t0, n=n: e.matmul(pr[:, 0:n], lhsT=self.rotP[:], rhs=self.hT[:, 0, t0:t0 + n], start=True, stop=True),
                             r=[Lqb, self.Lconst], w=[Lpr], c=PEC(n))
                    elif KV == 3:
                        s.op("pe", lambda e, pr=pr, t0=t0, n=n: e.matmul(pr[:, 0:n], lhsT=self.identb[:], rhs=self.hT[:, 0, t0:t0 + n], start=True, stop=True),
                             r=[Lqb, self.Lconst], w=[Lpr], c=PEC(n))
                    if KSUB < 3:
                        continue
                    t1 = self.vf32(tmp_o[0], 512)[:, 0:n]
                    t2 = self.vf32(tmp_o[1], 512)[:, 0:n]
                    s.op("dve", lambda e, t1=t1, ps=ps, t0=t0, n=n, rope=rope: e.tensor_tensor(out=t1, in0=ps[:, 0:n], in1=rope[:, 0, 0:n], op=ALU.mult),
                         r=[Lp, Lrope, Lqb], w=[tmp_L[0]])
                    s.op("dve", lambda e, t2=t2, pr=pr, t0=t0, n=n, rope=rope: e.tensor_tensor(out=t2, in0=pr[:, 0:n], in1=rope[:, 1, 0:n], op=ALU.mult),
                         r=[Lpr, Lrope], w=[tmp_L[1]])
                    if KSUB < 4:
                        continue
                    s.op("dve", lambda e, dstT=dstT, t1=t1, t2=t2, t0=t0, n=n: e.tensor_tensor(out=dstT[:, t0:t0 + n], in0=t1, in1=t2, op=ALU.add),
                         r=[tmp_L[0], tmp_L[1]], w=[dL[t]])
                    if t == 4 and ty == 0 and KSUB >= 5:
                        s.op("dve", lambda e, t1=t1, t2=t2: e.tensor_tensor(out=q32s, in0=t1, in1=t2, op=ALU.add),
                             r=[tmp_L[0], tmp_L[1]], w=[q32_L[0]])
                if KSUB < 5:
                    continue
                ps, Lp = self.next_ps()
                for k in range(8):
                    s.op("pe", lambda e, ps=ps, k=k, t0=t0, n=n, wv=wv: e.matmul(
                        ps[:, 0:n], lhsT=wv[:, 3, k, :], rhs=self.hT[:, k, t0:t0 + n], start=(k == 0), stop=(k == 7)),
                        r=[Lw, self.LhT[k][t]], w=[Lp], c=PEC(n))
                s.op("act", lambda e, ps=ps, t0=t0, n=n: e.activation(out=gate[:, t0:t0 + n], in_=ps[:, 0:n], func=AF.Silu), r=[Lp], w=[gt_L[t]])
            if KDBG < 2:
                self.wrelease(1)
                continue
            for t in range(5):
                ps, Lp = self.next_ps()
                nn = 4 if t < 4 else 1
                for j in range(nn):
                    tt = t * 4 + j
                    tok0 = tt * 128
                    nt = 128 if t < 4 else 64
                    for k in range(8):
                        s.op("pe", lambda e, ps=ps, j=j, k=k, tok0=tok0, nt=nt, wv=wv: e.matmul(
                            ps[0:nt, j * 128:(j + 1) * 128], lhsT=self.hT[:, k, tok0:tok0 + nt], rhs=wv[:, 2, k, :],
                            start=(k == 0), stop=(k == 7)), r=[Lw, self.LhT[k][t]], w=[Lp])
                if t < 4:
                    s.op("act", lambda e, ps=ps, t=t: e.activation(out=vtok[:, 4 * t:4 * t + 4, :], in_=ps[:].rearrange("p (n d) -> p n d", n=4),
                                                                   func=AF.Copy), r=[Lp], w=[v_L[t]])
                else:
                    s.op("act", lambda e, ps=ps: e.activation(out=vtok[0:64, 16, :], in_=ps[0:64, 0:128], func=AF.Copy), r=[Lp], w=[v_L[4]])
            if KDBG < 3:
                self.wrelease(1)
                continue
            for t in range(5):
                ps, Lp = self.next_ps()
                psb = ps[:].bitcast(BF16)
                if t < 4:
                    for j in range(4):
                        n_ = 4 * t + j
                        s.op("pe", lambda e, psb=psb, j=j, n_=n_: e.transpose(psb[:, j * 128:(j + 1) * 128], kT[:, n_ * 128:(n_ + 1) * 128], self.identb[:]),
                             r=[k_L[t], self.Lconst], w=[Lp])
                    s.op("dve", lambda e, psb=psb, t=t, hh=hh: e.tensor_scalar(
                        out=kdtok[:, 4 * t:4 * t + 4, :], in0=psb[:, 0:512].rearrange("p (n d) -> p n d", n=4),
                        scalar1=tabs[:, TC_KDEC + hh:TC_KDEC + hh + 1], scalar2=None, op0=ALU.mult), r=[Lp, self.Lconst], w=[kd_L[t]])
                else:
                    s.op("pe", lambda e, psb=psb: e.transpose(psb[0:64, 0:128], kT[:, SEQ:SEQ + 64], self.identb[:]),
                         r=[k_L[4], self.Lconst], w=[Lp])
                    s.op("dve", lambda e, psb=psb, hh=hh: e.tensor_scalar(
                        out=kdtok[0:64, 16, :], in0=psb[0:64, 0:128], scalar1=tabs[0:64, TC_KDECS + hh:TC_KDECS + hh + 1],
                        scalar2=None, op0=ALU.mult), r=[Lp, self.Lconst], w=[kd_L[4]])
            self.wrelease(1)
            if KDBG < 4:
                continue
            cd = float(self.gam[hh] ** 128)
            for t in range(4):
                ps, Lp = self.next_ps()
                for j in range(4):
                    n_ = 4 * t + j
                    s.op("pe", lambda e, ps=ps, j=j, n_=n_: e.matmul(ps[:, j * 128:(j + 1) * 128], lhsT=kdtok[:, n_, :], rhs=vtok[:, n_, :],
                                                                    start=True, stop=True), r=[kd_L[t], v_L[t]], w=[Lp])
                for j in range(4):
                    n_ = 4 * t + j
                    if n_ == 0:
                        s.op("dve", lambda e, ps=ps: e.tensor_copy(out=S32, in_=ps[:, 0:128]), r=[Lp], w=[s32_L[0]])
                    else:
                        s.op("dve", lambda e, ps=ps, j=j, cd=cd: e.scalar_tensor_tensor(out=S32, in0=S32, scalar=cd, in1=ps[:, j * 128:(j + 1) * 128],
                                                                                op0=ALU.mult, op1=ALU.add), r=[Lp], w=[s32_L[0]])
                    if n_ < 15:
                        s.op("act", lambda e, n_=n_: e.activation(out=Sbf[:, n_ + 1, :], in_=S32, func=AF.Copy), r=[s32_L[0]], w=[sb_L[n_ + 1]])
            s.dma("sp", lambda e, h, hh=hh: e.dma_start(out=d["nrp"][l, hh], in_=S32).then_inc(h, 16), dso, 1, r=[s32_L[0]], w=[LT()])
            if KDBG < 5:
                continue
            for t, (t0, n) in enumerate(TT):
                if KDBG < 6 and t == 4:
                    continue
                sT = self.vbf(st_o[t % 2], 512)
                LsT = st_L[t % 2]
                po, Lpo = self.next_ps()
                if t < 4:
                    ps, Lp = self.next_ps()
                    for j in range(4):
                        n_ = 4 * t + j
                        s.op("pe", lambda e, ps=ps, j=j, n_=n_: e.matmul(ps[:, j * 128:(j + 1) * 128], lhsT=kT[:, n_ * 128:(n_ + 1) * 128],
                                                                        rhs=qT[:, n_ * 128:(n_ + 1) * 128], start=True, stop=True),
                             r=[k_L[t], q_L[t]], w=[Lp])
                    s.op("dve", lambda e, ps=ps, sT=sT, hh=hh: e.tensor_tensor(
                        out=sT.rearrange("p (n i) -> p n i", n=4), in0=ps[:].rearrange("p (n i) -> p n i", n=4),
                        in1=tabs[:, TC_MASKP + hh * 128:TC_MASKP + (hh + 1) * 128].unsqueeze(1).to_broadcast([128, 4, 128]), op=ALU.mult),
                        r=[Lp, self.Lconst], w=[LsT])
                    for j in range(4):
                        n_ = 4 * t + j
                        s.op("pe", lambda e, po=po, j=j, n_=n_, sT=sT: e.matmul(po[:, j * 128:(j + 1) * 128], lhsT=vtok[:, n_, :],
                                                                               rhs=sT[:, j * 128:(j + 1) * 128], start=True, stop=(n_ == 0)),
                             r=[v_L[t], LsT], w=[Lpo])
                        if n_ > 0:
                            s.op("pe", lambda e, po=po, j=j, n_=n_: e.matmul(po[:, j * 128:(j + 1) * 128], lhsT=Sbf[:, n_, :],
                                                                            rhs=qT[:, n_ * 128:(n_ + 1) * 128], start=False, stop=True),
                                 r=[sb_L[n_], q_L[t]], w=[Lpo])
                    osb = self.vf32(tmp_o[2], 512)
                    s.op("dve", lambda e, po=po, osb=osb, hh=hh: e.tensor_tensor(
                        out=osb.rearrange("p (n i) -> p n i", n=4), in0=po[:].rearrange("p (n i) -> p n i", n=4),
                        in1=tabs[:, TC_QDEC + hh * 128:TC_QDEC + (hh + 1) * 128].unsqueeze(1).to_broadcast([128, 4, 128]), op=ALU.mult),
                        r=[Lpo, self.Lconst], w=[tmp_L[2]])
                else:
                    ps, Lp = self.next_ps()
                    s.op("pe", lambda e, ps=ps: e.matmul(ps[0:64, 0:64], lhsT=kT[:, SEQ:SEQ + 64], rhs=qT[:, SEQ:SEQ + 64], start=True, stop=True),
                         r=[k_L[4], q_L[4]], w=[Lp])
                    s.op("dve", lambda e, ps=ps, sT=sT, hh=hh: e.tensor_tensor(out=sT[0:64, 0:64], in0=ps[0:64, 0:64],
                                                                             in1=tabs[0:64, TC_MASKS + hh * 64:TC_MASKS + (hh + 1) * 64], op=ALU.mult),
                         r=[Lp, self.Lconst], w=[LsT])
                    s.op("pe", lambda e, po=po, sT=sT: e.matmul(po[:, 0:64], lhsT=vtok[0:64, 16, :], rhs=sT[0:64, 0:64], start=True, stop=True),
                         r=[v_L[4], LsT], w=[Lpo])
                    pi, Lpi = self.next_ps()
                    for b in range(16):
                        s.op("pe", lambda e, pi=pi, b=b: e.matmul(pi[:, 4 * b:4 * b + 4], lhsT=S0[:, b, :], rhs=q32s[:, 4 * b:4 * b + 4], start=True, stop=True),
                             r=[s0_L[0], q32_L[0]], w=[Lpi])
                    oi = self.vf32(tmp_o[1], 512)[:, 0:64]
                    s.op("act", lambda e, pi=pi, oi=oi: e.activation(out=oi, in_=pi[:, 0:64], func=AF.Copy), r=[Lpi], w=[tmp_L[1]])
                    osb = self.vf32(tmp_o[2], 512)
                    s.op("dve", lambda e, po=po, oi=oi, osb=osb: e.tensor_tensor(out=osb[:, 0:64], in0=po[:, 0:64], in1=oi, op=ALU.add),
                         r=[Lpo, tmp_L[1]], w=[tmp_L[2]])
                    s.op("dve", lambda e, osb=osb, hh=hh: e.tensor_tensor(out=osb[:, 0:64], in0=osb[:, 0:64],
                                                                        in1=tabs[:, TC_QDECS + hh * 64:TC_QDECS + (hh + 1) * 64], op=ALU.mult),
                         r=[self.Lconst], w=[tmp_L[2]])
                    self.phase_reset(keep=keep)
                    km_o, km_L = self.alloc(16 * 128 * 2, 1, "kmask")
                    kmask = self.vbf(km_o, 16 * 128).rearrange("p (b d) -> p b d", b=16)
                    s.op("dve", lambda e: e.tensor_tensor(
                        out=kmask[0:64], in0=kdtok[0:64, 16, :].unsqueeze(1).to_broadcast([64, 16, 128]),
                        in1=tabs[0:64, TC_ONEH:TC_ONEH + 16].unsqueeze(2).to_broadcast([64, 16, 128]), op=ALU.mult),
                        r=[kd_L[4], self.Lconst], w=[km_L[0]])
                    cd4 = float(self.gam[hh] ** 4)
                    for bg in range(4):
                        pk, Lpk = self.next_ps()
                        for j in range(4):
                            b = bg * 4 + j
                            s.op("pe", lambda e, pk=pk, j=j, b=b: e.matmul(pk[:, j * 128:(j + 1) * 128], lhsT=kmask[0:64, b, :], rhs=vtok[0:64, 16, :],
                                                                          start=True, stop=True), r=[km_L[0], v_L[4]], w=[Lpk])
                        s.op("dve", lambda e, pk=pk, bg=bg, cd4=cd4: e.scalar_tensor_tensor(
                            out=S0[:, 4 * bg:4 * bg + 4, :], in0=S0[:, 4 * bg:4 * bg + 4, :], scalar=cd4,
                            in1=pk[:].rearrange("p (b d) -> p b d", b=4), op0=ALU.mult, op1=ALU.add), r=[Lpk], w=[s0_L[0]])
                    def st_s0(e, h, hh=hh):
                        for bg in range(4):
                            e.dma_start(out=d["nrs"][l][4 * bg:4 * bg + 4, hh].rearrange("b k v -> k b v"), in_=S0[:, 4 * bg:4 * bg + 4, :]).then_inc(h, 16)
                    s.dma("sp", st_s0, dso_s, 4, r=[s0_L[0]], w=[LT()])
                osbn = self.vf32(tmp_o[2], 512)[:, 0:n]
                sq = self.vbf(tmp_o[0], 512)[:, 0:n]
                ob = self.vbf(tmp_o[0] + 1024, 512)[:, 0:n]
                s.op("act", lambda e, sq=sq, osbn=osbn: e.activation(out=sq, in_=osbn, func=AF.Square), r=[tmp_L[2]], w=[tmp_L[0]])
                s.op("act", lambda e, ob=ob, osbn=osbn: e.activation(out=ob, in_=osbn, func=AF.Copy), r=[tmp_L[2], tmp_L[0]], w=[tmp_L[0]])
                pm, Lm = self.next_ps()
                pe2, Le = self.next_ps()
                s.op("pe", lambda e, pm=pm, ob=ob, n=n: e.matmul(pm[:, 0:n], lhsT=self.ones_gnb[:], rhs=ob, start=True, stop=True),
                     r=[tmp_L[0], self.Lconst], w=[Lm], c=PEC(n))
                s.op("pe", lambda e, pe2=pe2, sq=sq, n=n: e.matmul(pe2[:, 0:n], lhsT=self.ones_gnb[:], rhs=sq, start=True, stop=True),
                     r=[tmp_L[0], self.Lconst], w=[Le], c=PEC(n))
                mean = self.vf32(tmp_o[3], 512)[:, 0:n]
                rstd = self.vf32(tmp_o[4], 512)[:, 0:n]
                self.norm_stats(pm, Lm, pe2, Le, mean, tmp_L[3], rstd, tmp_L[4], n, GN_EPS)
                s.op("dve", lambda e, osbn=osbn, mean=mean: e.tensor_tensor(out=osbn, in0=osbn, in1=mean, op=ALU.subtract), r=[tmp_L[3]], w=[tmp_L[2]])
                s.op("dve", lambda e, osbn=osbn, rstd=rstd: e.tensor_tensor(out=osbn, in0=osbn, in1=rstd, op=ALU.mult), r=[tmp_L[4]], w=[tmp_L[2]])
                s.op("dve", lambda e, osbn=osbn, hh=hh, t0=t0, n=n: e.scalar_tensor_tensor(
                    out=ro[:, hh, t0:t0 + n], in0=osbn, scalar=self.pc("gn_g", l, hh), in1=gate[:, t0:t0 + n], op0=ALU.mult, op1=ALU.mult),
                    r=[tmp_L[2], gt_L[t], self.Lconst], w=[ro_L[hh * 5 + t]])
        if KDBG < 6:
            return
        self.proj_acc_rows("w_out", l, 512, lambda k, t: (ro[:, k, TT[t][0]:TT[t][0] + TT[t][1]], ro_L[k * 5 + t]), 1.0)

    def crossattn(self, l):
        s, d = self.s, self.d
        self.rmsnorm("g_ca", l)
        self.phase_reset()
        ocs_o, ocs_L = self.alloc(8 * 64 * 2, 1, "ocas")
        ocas = self.vbf(ocs_o, 8 * 64).rearrange("p (c t) -> p c t", c=8)
        qca_o, qca_L = self.alloc(8 * SEQ * 2, 32, "qca")
        qca = self.vbf(qca_o, 8 * SEQ).rearrange("p (c t) -> p c t", c=8)
        keep = self.aoff
        qcs_o, qcs_L = self.alloc(8 * 64 * 2, 8, "qcas")
        qcas = self.vbf(qcs_o, 8 * 64).rearrange("p (c t) -> p c t", c=8)
        NKV = 3
        kb_o, kb_L, vb_o, vb_L = [], [], [], []
        for i in range(NKV):
            o, L = self.alloc(4096, 1, "kb"); kb_o.append(o); kb_L.append(L[0])
            o, L = self.alloc(4096, 1, "vb"); vb_o.append(o); vb_L.append(L[0])
        kTb_o, kTb_L = [], []
        for i in range(2):
            o, L = self.alloc(4096, 1, "kTb"); kTb_o.append(o); kTb_L.append(L[0])
        pT_o, pT_L = [], []
        for i in range(2):
            o, L = self.alloc(2048, 1, "pT"); pT_o.append(o); pT_L.append(L[0])
        rd_o, rd_L = [], []
        for i in range(2):
            o, L = self.alloc(2048, 1, "rden"); rd_o.append(o); rd_L.append(L[0])
        dmem = s.dsem("mem%d" % l)
        dkv = [s.dsem("kv%d_%d" % (l, i)) for i in range(NKV)]
        dmo = [s.dsem("mo%d_%d" % (l, i)) for i in range(2)]
        wq = [self.wblock(("A", "w_cq", l, 0)), self.wblock(("A", "w_cq", l, 512))]

        def qgroup(half, t, ql, dstq, dstL, col0):
            ring, Lw = wq[half]
            wv = ring[:].rearrange("p (k n) -> p k n", k=8)
            t0, n = TT[t]
            qc = half * 4 + ql
            ps, Lp = self.next_ps()
            for k in range(8):
                s.op("pe", lambda e, ps=ps, wv=wv, k=k, ql=ql, t0=t0, n=n: e.matmul(
                    ps[:, 0:n], lhsT=wv[:, k, ql * 128:(ql + 1) * 128], rhs=self.hT[:, k, t0:t0 + n], start=(k == 0), stop=(k == 7)),
                    r=[Lw, self.LhT[k][t]], w=[Lp], c=PEC(n))
            s.op("act", lambda e, ps=ps, qc=qc, t0=t0, n=n: e.activation(out=dstq[:, qc, t0 - col0:t0 - col0 + n], in_=ps[:, 0:n], func=AF.Copy),
                 r=[Lp], w=[dstL(qc, t)])

        def qproj(tiles, dstq, dstL, col0):
            for half in range(2):
                for t in tiles:
                    for ql in range(4):
                        qgroup(half, t, ql, dstq, dstL, col0)
        pq = [(half, t, ql) for half in range(2) for t in range(4) for ql in range(4)]
        qproj([4], qcas, lambda qc, t: qcs_L[qc], SEQ)

        def load_kv(b):
            sl = b % NKV
            kb = self.vbf(kb_o[sl], 2048).rearrange("p (c d) -> p c d", c=2)
            vb = self.vbf(vb_o[sl], 2048).rearrange("p (c d) -> p c d", c=2)

            def fn(e, h):
                e.dma_start(out=kb, in_=d["cmk"][l, b].rearrange("(c p) d -> p c d", p=128)).then_inc(h, 16)
                e.dma_start(out=vb, in_=d["cmv"][l, b].rearrange("(c p) d -> p c d", p=128)).then_inc(h, 16)
            s.dma("pool", fn, dkv[sl], 2, w=[kb_L[sl], vb_L[sl]], c=9.0)
        for b in range(NKV - 1):
            load_kv(b)
        for b in range(16):
            if b + NKV - 1 < 16:
                load_kv(b + NKV - 1)
            sl = b % NKV
            kTb = self.vbf(kTb_o[b % 2], 2048).rearrange("p (c m) -> p c m", c=8)
            LkTb = kTb_L[b % 2]
            kb = self.vbf(kb_o[sl], 2048).rearrange("p (c d) -> p c d", c=2)
            vb = self.vbf(vb_o[sl], 2048).rearrange("p (c d) -> p c d", c=2)
            for mc in range(2):
                ps, Lp = self.next_ps()
                psb = ps[:].bitcast(BF16)
                for c in range(8):
                    s.op("pe", lambda e, psb=psb, mc=mc, c=c, kb=kb: e.transpose(psb[:, c * 128:(c + 1) * 128], kb[:, mc, c * 128:(c + 1) * 128], self.identb[:]),
                         r=[kb_L[sl], self.Lconst], w=[Lp])
                eng = "act" if mc == 0 else "dve"
                if eng == "act":
                    s.op("act", lambda e, psb=psb, mc=mc, kTb=kTb: e.activation(out=kTb[:, :, mc * 128:(mc + 1) * 128], in_=psb.rearrange("p (c m) -> p c m", c=8), func=AF.Copy),
                         r=[Lp], w=[LkTb])
                else:
                    s.op("dve", lambda e, psb=psb, mc=mc, kTb=kTb: e.tensor_copy(out=kTb[:, :, mc * 128:(mc + 1) * 128], in_=psb.rearrange("p (c m) -> p c m", c=8)),
                         r=[Lp], w=[LkTb])
            ps, Lp = self.next_ps()
            for hh in range(4):
                for mc in range(2):
                    col = (mc * 4 + hh) * 4
                    for hf in range(2):
                        s.op("pe", lambda e, ps=ps, col=col, hh=hh, mc=mc, hf=hf, b=b, kTb=kTb: e.matmul(
                            ps[:, col:col + 4], lhsT=kTb[:, 2 * hh + hf, mc * 128:(mc + 1) * 128], rhs=qcas[:, 2 * hh + hf, 4 * b:4 * b + 4],
                            start=(hf == 0), stop=(hf == 1)), r=[LkTb, qcs_L[2 * hh + hf]], w=[Lp], c=0.06)
            pT = self.vbf(pT_o[b % 2], 1024)[:, 0:32]
            LpT = pT_L[b % 2]
            s.op("act", lambda e, pT=pT, ps=ps: e.activation(out=pT, in_=ps[:, 0:32], func=AF.Exp, scale=1.0 / 16), r=[Lp], w=[LpT])
            pden, Lpd = self.next_ps()
            for mc in range(2):
                s.op("pe", lambda e, pden=pden, pT=pT, mc=mc: e.matmul(pden[:, 0:16], lhsT=self.ones_b[:], rhs=pT[:, mc * 16:(mc + 1) * 16],
                                                                       start=(mc == 0), stop=(mc == 1)), r=[LpT, self.Lconst], w=[Lpd])
            rden = self.vf32(rd_o[b % 2], 512)[:, 0:16]
            Lrd = rd_L[b % 2]
            s.op("dve", lambda e, rden=rden, pden=pden: e.reciprocal(out=rden, in_=pden[:, 0:16]), r=[Lpd], w=[Lrd])
            po, Lpo = self.next_ps()
            for hh in range(4):
                for dh in range(2):
                    col = (hh * 2 + dh) * 4
                    for mc in range(2):
                        s.op("pe", lambda e, po=po, col=col, hh=hh, dh=dh, mc=mc, vb=vb, pT=pT: e.matmul(
                            po[:, col:col + 4], lhsT=vb[:, mc, hh * 256 + dh * 128:hh * 256 + (dh + 1) * 128], rhs=pT[:, (mc * 4 + hh) * 4:(mc * 4 + hh) * 4 + 4],
                            start=(mc == 0), stop=(mc == 1)), r=[vb_L[sl], LpT], w=[Lpo])
            s.op("dve", lambda e, po=po, rden=rden, b=b: e.tensor_tensor(
                out=ocas[:, :, 4 * b:4 * b + 4].rearrange("p (h d) l -> p h d l", h=4),
                in0=po[:, 0:32].rearrange("p (h d l) -> p h d l", h=4, d=2),
                in1=rden.rearrange("p (h l) -> p h l", h=4).unsqueeze(2).to_broadcast([128, 4, 2, 4]), op=ALU.mult),
                r=[Lpo, Lrd], w=[ocs_L[0]])
            for _ in range(2):
                if pq:
                    g_ = pq.pop(0)
                    qgroup(g_[0], g_[1], g_[2], qca, lambda qc, t: qca_L[qc * 4 + t], 0)
        while pq:
            g_ = pq.pop(0)
            qgroup(g_[0], g_[1], g_[2], qca, lambda qc, t: qca_L[qc * 4 + t], 0)
        self.wrelease(2)
        self.phase_reset(keep=keep)
        kTm_o, kTm_L = self.alloc(8 * 256 * 2, 1, "kTm")
        kTm = self.vbf(kTm_o, 8 * 256).rearrange("p (c m) -> p c m", c=8)
        vm_o, vm_L = self.alloc(2 * 1024 * 2, 1, "vm")
        vm = self.vbf(vm_o, 2048).rearrange("p (c d) -> p c d", c=2)
        mnT_o, mnT_L = self.alloc(8 * 256 * 2, 1, "mnT")
        mnT = self.vbf(mnT_o, 8 * 256).rearrange("p (c m) -> p c m", c=8)
        mst_o, mst_L = self.alloc(2 * 1024 * 4, 1, "memst")
        mst = self.vf32(mst_o, 2048).rearrange("p (c d) -> p c d", c=2)
        sm_o, sm_L = self.alloc(1024, 1, "small")
        ost_o, ost_L = [], []
        for i in range(2):
            o, L = self.alloc(2048, 1, "ost"); ost_o.append(o); ost_L.append(L[0])
        pT_o, pT_L = [], []
        for i in range(2):
            o, L = self.alloc(2048, 1, "pT2"); pT_o.append(o); pT_L.append(L[0])
        rd_o, rd_L = [], []
        for i in range(2):
            o, L = self.alloc(2048, 1, "rden2"); rd_o.append(o); rd_L.append(L[0])
        s.dma("sp", lambda e, h: e.dma_start(out=mst, in_=d["memp"].rearrange("(c p) d -> p c d", p=128)).then_inc(h, 16), dmem, 1, w=[mst_L[0]])
        small = self.vf32(sm_o, 256)
        junk = self.vf32(ost_o[0], 512)
        for mc in range(2):
            for hf in range(2):
                s.op("act", lambda e, mc=mc, hf=hf: e.activation(out=junk, in_=mst[:, mc, hf * 512:(hf + 1) * 512], func=AF.Square,
                                                                 accum_out=small[:, mc * 2 + hf:mc * 2 + hf + 1]), r=[mst_L[0]], w=[ost_L[0], sm_L[0]])
        s.op("dve", lambda e: e.tensor_tensor(out=small[:, 4:6], in0=small[:, 0:4].rearrange("p (a b) -> p a b", b=2)[:, :, 0],
                                              in1=small[:, 0:4].rearrange("p (a b) -> p a b", b=2)[:, :, 1], op=ALU.add), r=[sm_L[0]], w=[sm_L[0]])
        s.op("dve", lambda e: e.tensor_scalar(out=small[:, 4:6], in0=small[:, 4:6], scalar1=1.0 / 1024, scalar2=None, op0=ALU.mult),
             r=[sm_L[0]], w=[sm_L[0]])
        s.op("act", lambda e: e.activation(out=small[:, 4:6], in_=small[:, 4:6], func=AF.Sqrt, bias=EPS, scale=1.0), r=[sm_L[0]], w=[sm_L[0]])
        s.op("dve", lambda e: e.reciprocal(out=small[:, 4:6], in_=small[:, 4:6]), r=[sm_L[0]], w=[sm_L[0]])
        for mc in range(2):
            s.op("dve", lambda e, mc=mc: e.tensor_scalar(out=mst[:, mc, :], in0=mst[:, mc, :], scalar1=small[:, 4 + mc:5 + mc], scalar2=None, op0=ALU.mult),
                 r=[sm_L[0]], w=[mst_L[0]])
        for c in range(8):
            ps, Lp = self.next_ps()
            for mc in range(2):
                s.op("pe", lambda e, ps=ps, mc=mc, c=c: e.transpose(ps[:, mc * 128:(mc + 1) * 128], mst[:, mc, c * 128:(c + 1) * 128], self.identf[:]),
                     r=[mst_L[0], self.Lconst], w=[Lp])
            s.op("dve", lambda e, ps=ps, c=c: e.tensor_scalar(out=mnT[:, c, :], in0=ps[:, 0:256], scalar1=self.pc("g_mem", l, c), scalar2=None, op0=ALU.mult),
                 r=[Lp, self.Lconst], w=[mnT_L[0]])
        wk = [self.wblock(("A", "w_ck", l, 0)), self.wblock(("A", "w_ck", l, 512))]
        for half in range(2):
            ring, Lw = wk[half]
            wv = ring[:].rearrange("p (k n) -> p k n", k=8)
            for ql in range(4):
                c = half * 4 + ql
                ps, Lp = self.next_ps()
                for k in range(8):
                    s.op("pe", lambda e, ps=ps, wv=wv, k=k, ql=ql: e.matmul(ps[:, 0:256], lhsT=wv[:, k, ql * 128:(ql + 1) * 128], rhs=mnT[:, k, :],
                                                                           start=(k == 0), stop=(k == 7)), r=[Lw, mnT_L[0]], w=[Lp])
                s.op("act", lambda e, ps=ps, c=c: e.activation(out=kTm[:, c, :], in_=ps[:, 0:256], func=AF.Copy), r=[Lp], w=[kTm_L[0]])
            for mc in range(2):
                ps, Lp = self.next_ps()
                for k in range(8):
                    s.op("pe", lambda e, ps=ps, wv=wv, k=k, mc=mc: e.matmul(ps[:], lhsT=mnT[:, k, mc * 128:(mc + 1) * 128], rhs=wv[:, k, :],
                                                                           start=(k == 0), stop=(k == 7)), r=[Lw, mnT_L[0]], w=[Lp])
                oi = (half * 2 + mc) % 2
                ost = self.vf32(ost_o[oi], 512)
                s.op("act", lambda e, ps=ps, ost=ost: e.activation(out=ost, in_=ps[:], func=AF.Copy), r=[Lp], w=[ost_L[oi]])
                s.dma("sp", lambda e, h, ost=ost, mc=mc, half=half: e.dma_start(out=d["nmk"][l, mc * 128:(mc + 1) * 128, half * 512:(half + 1) * 512], in_=ost).then_inc(h, 16),
                      dmo[oi], 1, r=[ost_L[oi]], w=[LT()])
        self.wrelease(2)
        wvv = [self.wblock(("A", "w_cv", l, 0)), self.wblock(("A", "w_cv", l, 512))]
        for half in range(2):
            ring, Lw = wvv[half]
            wv = ring[:].rearrange("p (k n) -> p k n", k=8)
            for mc in range(2):
                ps, Lp = self.next_ps()
                for k in range(8):
                    s.op("pe", lambda e, ps=ps, wv=wv, k=k, mc=mc: e.matmul(ps[:], lhsT=mnT[:, k, mc * 128:(mc + 1) * 128], rhs=wv[:, k, :],
                                                                           start=(k == 0), stop=(k == 7)), r=[Lw, mnT_L[0]], w=[Lp])
                oi = (half * 2 + mc) % 2
                ost = self.vf32(ost_o[oi], 512)
                s.op("act", lambda e, ps=ps, ost=ost: e.activation(out=ost, in_=ps[:], func=AF.Copy), r=[Lp], w=[ost_L[oi]])
                s.op("dve", lambda e, ost=ost, mc=mc, half=half: e.tensor_copy(out=vm[:, mc, half * 512:(half + 1) * 512], in_=ost), r=[ost_L[oi]], w=[vm_L[0]])
                s.dma("sp", lambda e, h, ost=ost, mc=mc, half=half: e.dma_start(out=d["nmv"][l, mc * 128:(mc + 1) * 128, half * 512:(half + 1) * 512], in_=ost).then_inc(h, 16),
                      dmo[oi], 1, r=[ost_L[oi]], w=[LT()])
        self.wrelease(2)
        pi_ = 0
        for hh in range(4):
            for t in range(4):
                t0, n = TT[t]
                pT = self.vbf(pT_o[pi_ % 2], 1024).rearrange("p (c t) -> p c t", c=2)
                LpT = pT_L[pi_ % 2]
                rden = self.vf32(rd_o[pi_ % 2], 512)
                Lrd = rd_L[pi_ % 2]
                pi_ += 1
                for mc in range(2):
                    ps, Lp = self.next_ps()
                    for hf in range(2):
                        s.op("pe", lambda e, ps=ps, hh=hh, mc=mc, hf=hf, t0=t0: e.matmul(
                            ps[:], lhsT=kTm[:, 2 * hh + hf, mc * 128:(mc + 1) * 128], rhs=qca[:, 2 * hh + hf, t0:t0 + 512],
                            start=(hf == 0), stop=(hf == 1)), r=[kTm_L[0], qca_L[(2 * hh + hf) * 4 + t]], w=[Lp])
                    s.op("act", lambda e, ps=ps, pT=pT, mc=mc: e.activation(out=pT[:, mc, :], in_=ps[:], func=AF.Exp, scale=1.0 / 16), r=[Lp], w=[LpT])
                pden, Lpd = self.next_ps()
                for mc in range(2):
                    s.op("pe", lambda e, pden=pden, pT=pT, mc=mc: e.matmul(pden[:], lhsT=self.ones_b[:], rhs=pT[:, mc, :], start=(mc == 0), stop=(mc == 1)),
                         r=[LpT, self.Lconst], w=[Lpd])
                s.op("act", lambda e, rden=rden, pden=pden: e.activation(out=rden, in_=pden[:], func=AF.Ln), r=[Lpd], w=[Lrd])
                s.op("act", lambda e, rden=rden: e.activation(out=rden, in_=rden, func=AF.Exp, scale=-1.0), r=[Lrd], w=[Lrd])
                for dh in range(2):
                    po, Lpo = self.next_ps()
                    for mc in range(2):
                        s.op("pe", lambda e, po=po, hh=hh, dh=dh, mc=mc, pT=pT: e.matmul(
                            po[:], lhsT=vm[:, mc, hh * 256 + dh * 128:hh * 256 + (dh + 1) * 128], rhs=pT[:, mc, :], start=(mc == 0), stop=(mc == 1)),
                            r=[vm_L[0], LpT], w=[Lpo])
                    c = 2 * hh + dh
                    s.op("dve", lambda e, po=po, rden=rden, c=c, t0=t0: e.tensor_tensor(out=self.hT[:, c, t0:t0 + 512], in0=po[:], in1=rden, op=ALU.mult),
                         r=[Lpo, Lrd], w=[self.LhT[c][t]])
        def oin(k, t):
            if t < 4:
                return self.hT[:, k, TT[t][0]:TT[t][0] + 512], self.LhT[k][t]
            return ocas[:, k, :], ocs_L[0]
        self.proj_acc([("A", "w_co", l, 0), ("A", "w_co", l, 512)], oin, 8, 1.0)

    def final(self, raw=False):
        s, d = self.s, self.d
        self.phase_reset()
        NF = 4
        yT_o, yT_L = [], []
        for i in range(2):
            o, L = self.alloc(8 * 512 * 4, 4, "yT"); yT_o.append(o); yT_L.append(L)
        og_o, og_L = [], []
        for i in range(NF):
            o, L = self.alloc(4096, 1, "ostg"); og_o.append(o); og_L.append(L[0])
        sq_o, sq_L = [], []
        for i in range(3):
            o, L = self.alloc(1024, 1, "fsq"); sq_o.append(o); sq_L.append(L[0])
        rs_o, rs_L = [], []
        for i in range(2):
            o, L = self.alloc(2048, 1, "frstd"); rs_o.append(o); rs_L.append(L[0])
        dso = [s.dsem("yout%d" % i) for i in range(NF)]
        qi = 0
        oi = 0
        for t, (t0, n) in enumerate(TT):
            yT = self.vf32(yT_o[t % 2], 8 * 512).rearrange("p (c t) -> p c t", c=8)
            LyT4 = yT_L[t % 2]
            if not raw:
                ps, Lp = self.next_ps()
                for c in range(8):
                    sq = self.vbf(sq_o[qi % 3], 512)[:, 0:n]
                    Lq = sq_L[qi % 3]
                    qi += 1
                    s.op("act", lambda e, sq=sq, c=c, t0=t0, n=n: e.activation(out=sq, in_=self.xT[:, c, t0:t0 + n], func=AF.Square),
                         r=[self.LxT[c][t]], w=[Lq])
                    s.op("pe", lambda e, ps=ps, sq=sq, c=c, n=n: e.matmul(ps[:, 0:n], lhsT=self.ones_rms[:], rhs=sq, start=(c == 0), stop=(c == 7)),
                         r=[Lq, self.Lconst], w=[Lp], c=PEC(n))
                rstd = self.vf32(rs_o[t % 2], 512)[:, 0:n]
                Lr = rs_L[t % 2]
                s.op("act", lambda e, rstd=rstd, ps=ps, n=n: e.activation(out=rstd, in_=ps[:, 0:n], func=AF.Ln, bias=EPS, scale=1.0), r=[Lp], w=[Lr])
                s.op("act", lambda e, rstd=rstd: e.activation(out=rstd, in_=rstd, func=AF.Exp, scale=-0.5), r=[Lr], w=[Lr])
                for c in range(8):
                    s.op("dve", lambda e, c=c, t0=t0, n=n, rstd=rstd, yT=yT: e.scalar_tensor_tensor(
                        out=yT[:, c, 0:n], in0=self.xT[:, c, t0:t0 + n], scalar=self.pc("g_final", 0, c), in1=rstd,
                        op0=ALU.mult, op1=ALU.mult), r=[self.LxT[c][t], Lr, self.Lconst] + LyT4, w=LyT4)
            else:
                s.op("dve", lambda e, t0=t0, n=n, yT=yT: e.tensor_copy(out=yT[:, :, 0:n], in_=self.xT[:, :, t0:t0 + n]),
                     r=[self.LxT[c][t] for c in range(8)], w=LyT4)
            nsub = 4 if t < 4 else 1
            for j4 in range(nsub):
                nn = 128 if t < 4 else 64
                tok0 = t0 + j4 * 128
                og = self.vf32(og_o[oi % NF], 1024)
                Log = og_L[oi % NF]
                dsem = dso[oi % NF]
                oi += 1
                for half in range(2):
                    ps, Lp = self.next_ps()
                    for j in range(4):
                        c = half * 4 + j
                        s.op("pe", lambda e, ps=ps, j=j, c=c, yT=yT, nn=nn, j4=j4: e.transpose(
                            ps[0:nn, j * 128:(j + 1) * 128], yT[:, c, j4 * 128:j4 * 128 + nn], self.identf[:]),
                            r=[LyT4[j4], self.Lconst], w=[Lp], c=0.25)
                    if half == 0:
                        s.op("act", lambda e, ps=ps, og=og, nn=nn: e.activation(out=og[0:nn, 0:512], in_=ps[0:nn, :], func=AF.Copy), r=[Lp], w=[Log])
                    else:
                        s.op("dve", lambda e, ps=ps, og=og, nn=nn: e.tensor_copy(out=og[0:nn, 512:1024], in_=ps[0:nn, :]), r=[Lp], w=[Log])
                dst = d["yp"][tok0:tok0 + 128, :] if t < 4 else d["ys"]
                s.dma("sp", lambda e, h, og=og, dst=dst, nn=nn: e.dma_start(out=dst, in_=og[0:nn, :]).then_inc(h, 16), dsem, 1, r=[Log], w=[LT()])

    def emit_all(self):
        sa = self.stop_after
        self.load_consts()
        self.load_x()
        step = 0

        def done():
            nonlocal step
            step += 1
            return sa is not None and step > sa
        for l in range(2):
            if done(): break
            self.rmsnorm("g_ffn1", l); self.ffn(l, "a")
            if done(): break
            self.rmsnorm("g_mix", l); self.conv(l)
            if done(): break
            self.retention(l)
            if done(): break
            self.crossattn(l)
            if done(): break
            self.rmsnorm("g_ffn2", l); self.ffn(l, "b")
        self.final(raw=(sa is not None))


def build_program(stop_after=None):
    wplan = None
    for _pass in range(2):
        nc = bass.Bass("TRN2", target_bir_lowering=False)
        s = Sched()
        with contextlib.ExitStack() as st:
            K = Kern(nc, s, st, wplan, stop_after)
            K.emit_all()
            if _pass == 0:
                wplan = K.wplan_out
                continue
            est = s.reorder() if REORDER else None
            cnt = s.emit(nc)
    return nc, (len(s.ops), cnt, est)


_CACHE = {}


def make_in_maps(inp):
    rope, tabs, mats, _ = const_tables()
    params = pack_params(inp)
    f = lambda a: np.ascontiguousarray(np.asarray(a, np.float32))
    shared = {"w1a": f(inp["w1_ffn1"]), "w3a": f(inp["w3_ffn1"]), "w2a": f(inp["w2_ffn1"]), "w_in": f(inp["w_in"]),
              "w_out": f(inp["w_out"]), "w_cq": f(inp["w_cq"]), "w_ck": f(inp["w_ck"]), "w_cv": f(inp["w_cv"]),
              "w_co": f(inp["w_co"]), "w1b": f(inp["w1_ffn2"]), "w3b": f(inp["w3_ffn2"]), "w2b": f(inp["w2_ffn2"]),
              "params": params, "rope": rope, "tabs": tabs, "mats": mats}
    maps = []
    for c in range(8):
        b0 = c * NB_S
        m = dict(shared)
        m["xp"] = f(inp["x_prompt"][c])
        m["xs"] = f(np.asarray(inp["x_sample"])[b0:b0 + NB_S].reshape(NS_TOK, D))
        m["sconv"] = f(np.asarray(inp["state_conv"])[:, b0:b0 + NB_S])
        m["sret"] = f(np.asarray(inp["state_ret"])[:, b0:b0 + NB_S])
        m["cmk"] = f(np.asarray(inp["cache_mem_k"])[:, b0:b0 + NB_S].reshape(2, NB_S, NMEM, D))
        m["cmv"] = f(np.asarray(inp["cache_mem_v"])[:, b0:b0 + NB_S].reshape(2, NB_S, NMEM, D))
        m["memp"] = f(inp["mem_prompt"][c])
        maps.append(m)
    return maps


def assemble(res):
    r = res
    y_prompt = np.stack([r[c]["yp"] for c in range(8)], 0)
    y_sample = np.concatenate([r[c]["ys"].reshape(NB_S, 4, D) for c in range(8)], 0)
    ncp = np.stack([r[c]["ncp"] for c in range(8)], 1)
    nrp = np.stack([r[c]["nrp"] for c in range(8)], 1)
    nmk = np.stack([r[c]["nmk"].reshape(2, NMEM, 4, 256) for c in range(8)], 1)
    nmv = np.stack([r[c]["nmv"].reshape(2, NMEM, 4, 256) for c in range(8)], 1)
    ncs = np.concatenate([r[c]["ncs"] for c in range(8)], 1)
    nrs = np.concatenate([r[c]["nrs"] for c in range(8)], 1)
    return tuple(np.ascontiguousarray(a, dtype=np.float32) for a in (y_prompt, y_sample, ncp, nrp, nmk, nmv, ncs, nrs))


def kernel(**inputs):
    if "nc" not in _CACHE:
        _CACHE["nc"] = build_program()[0]
    nc = _CACHE["nc"]
    maps = make_in_maps(inputs)
    res = run_bass_kernel_spmd(nc, maps, core_ids=list(range(8)))
    return assemble(res.results)
```
